# Optimizing a Trainium2 kernel written in Bass

```python
import math
import jax, jax.numpy as jnp
from jax import lax
import numpy as np

D_MODEL = 1024
BATCH = 4
SEQ = 4096
DEPTH = 2

MEM_LEN = 256
HEAD_DIM = 64
CROSS_HEADS = 4
CROSS_WIDTH = CROSS_HEADS * HEAD_DIM
TOKEN_WIDTH = D_MODEL - CROSS_WIDTH
FOX_HEADS = TOKEN_WIDTH // HEAD_DIM
CONV_CHANNELS = TOKEN_WIDTH
CONV_WIDTH = 31
FFN_DIM = 256 * math.ceil(8 * D_MODEL / 3 / 256)
FFN_CONV_WIDTH = 3
Q_BLOCK = 128
N_FOX_LAYERS = (DEPTH + 1) // 2
N_CONV_LAYERS = DEPTH // 2
FOX_IN = 3 * TOKEN_WIDTH + FOX_HEADS + CROSS_WIDTH
CONV_IN = 2 * CONV_CHANNELS + CROSS_WIDTH
EPS = 1e-6
NEG = -1e30

kernel_name = "fox_conformer_interleaved_hybrid"


def rmsnorm(x, g):
    xf = x.astype(jnp.float32)
    y = xf * lax.rsqrt(jnp.mean(xf * xf, axis=-1, keepdims=True) + EPS)
    return (y * g.astype(jnp.float32)).astype(x.dtype)


def layernorm(x, g, b):
    xf = x.astype(jnp.float32)
    mu = jnp.mean(xf, axis=-1, keepdims=True)
    xc = xf - mu
    y = xc * lax.rsqrt(jnp.mean(xc * xc, axis=-1, keepdims=True) + EPS)
    return (y * g.astype(jnp.float32) + b.astype(jnp.float32)).astype(x.dtype)


def causal_dwconv(x, w):
    k_w, c = w.shape
    return lax.conv_general_dilated(
        x, w[:, None, :].astype(x.dtype), window_strides=(1,), padding=[(k_w - 1, 0)],
        dimension_numbers=('NWC', 'WIO', 'NWC'), feature_group_count=c)


def cross_attention(q, k, v):
    scale = 1.0 / math.sqrt(q.shape[-1])
    s = jnp.einsum('bshd,bmhd->bhsm', q, k).astype(jnp.float32) * scale
    p = jax.nn.softmax(s, axis=-1)
    return jnp.einsum('bhsm,bmhd->bshd', p.astype(v.dtype), v)


def fox_attention(q, k, v, log_f):
    scale = 1.0 / math.sqrt(q.shape[-1])
    c = jnp.transpose(jnp.cumsum(log_f, axis=1), (0, 2, 1))
    n_blocks = q.shape[1] // Q_BLOCK
    outs = []
    for blk in range(n_blocks):
        qs = blk * Q_BLOCK
        qe = qs + Q_BLOCK
        s = jnp.einsum('bqhd,bkhd->bhqk', q[:, qs:qe], k[:, :qe]).astype(jnp.float32) * scale
        bias = c[:, :, qs:qe, None] - c[:, :, None, :qe]
        mask = jnp.arange(qe)[None, :] <= (qs + jnp.arange(Q_BLOCK))[:, None]
        s = jnp.where(mask, s + bias, NEG)
        p = jax.nn.softmax(s, axis=-1)
        outs.append(jnp.einsum('bhqk,bkhd->bqhd', p.astype(v.dtype), v[:, :qe]))
    return jnp.concatenate(outs, axis=1)


def setup_inputs(seed: int = 0) -> dict:
    key = jax.random.key(seed)
    ks = jax.random.split(key, 24)
    f32 = jnp.float32

    def nrm(k, shape, scale):
        return jax.random.normal(k, shape, f32) * scale

    def gain(k, shape):
        return 1.0 + 0.1 * jax.random.normal(k, shape, f32)

    return {
        "x": nrm(ks[0], (BATCH, SEQ, D_MODEL), 1.0),
        "mem": nrm(ks[1], (BATCH, MEM_LEN, D_MODEL), 1.0),
        "mem_norm_g": gain(ks[2], (D_MODEL,)),
        "mem_w_kv": nrm(ks[3], (D_MODEL, 2 * CROSS_WIDTH), D_MODEL ** -0.5),
        "mix_norm_g": gain(ks[4], (DEPTH, D_MODEL)),
        "mix_w_out": nrm(ks[5], (DEPTH, D_MODEL, D_MODEL), D_MODEL ** -0.5),
        "cross_q_g": gain(ks[6], (DEPTH, HEAD_DIM)),
        "cross_k_g": gain(ks[7], (DEPTH, HEAD_DIM)),
        "fox_w_in": nrm(ks[8], (N_FOX_LAYERS, D_MODEL, FOX_IN), D_MODEL ** -0.5),
        "fox_b_f": 3.0 + 0.5 * jax.random.normal(ks[9], (N_FOX_LAYERS, FOX_HEADS), f32),
        "fox_q_g": gain(ks[10], (N_FOX_LAYERS, HEAD_DIM)),
        "fox_k_g": gain(ks[11], (N_FOX_LAYERS, HEAD_DIM)),
        "conv_w_in": nrm(ks[12], (N_CONV_LAYERS, D_MODEL, CONV_IN), D_MODEL ** -0.5),
        "conv_dw": nrm(ks[13], (N_CONV_LAYERS, CONV_WIDTH, CONV_CHANNELS), CONV_WIDTH ** -0.5),
        "conv_dw_b": nrm(ks[14], (N_CONV_LAYERS, CONV_CHANNELS), 0.02),
        "conv_ln_g": gain(ks[15], (N_CONV_LAYERS, CONV_CHANNELS)),
        "conv_ln_b": nrm(ks[16], (N_CONV_LAYERS, CONV_CHANNELS), 0.02),
        "ffn_norm_g": gain(ks[17], (DEPTH, D_MODEL)),
        "ffn_w_up": nrm(ks[18], (DEPTH, D_MODEL, 2 * FFN_DIM), D_MODEL ** -0.5),
        "ffn_conv": nrm(ks[19], (DEPTH, FFN_CONV_WIDTH, 2 * FFN_DIM), FFN_CONV_WIDTH ** -0.5),
        "ffn_w_down": nrm(ks[20], (DEPTH, FFN_DIM, D_MODEL), FFN_DIM ** -0.5),
    }


def reference(x, mem, mem_norm_g, mem_w_kv, mix_norm_g, mix_w_out, cross_q_g, cross_k_g,
              fox_w_in, fox_b_f, fox_q_g, fox_k_g,
              conv_w_in, conv_dw, conv_dw_b, conv_ln_g, conv_ln_b,
              ffn_norm_g, ffn_w_up, ffn_conv, ffn_w_down):
    b, s, _ = x.shape
    m = mem.shape[1]

    mem_kv = rmsnorm(mem, mem_norm_g) @ mem_w_kv
    k_mem, v_mem = jnp.split(mem_kv, 2, axis=-1)
    k_mem = k_mem.reshape(b, m, CROSS_HEADS, HEAD_DIM)
    v_mem = v_mem.reshape(b, m, CROSS_HEADS, HEAD_DIM)

    for i in range(DEPTH):
        j = i // 2
        h = rmsnorm(x, mix_norm_g[i])
        if i % 2 == 0:
            proj = h @ fox_w_in[j]
            q, k, v, f_logit, cq = jnp.split(
                proj, [TOKEN_WIDTH, 2 * TOKEN_WIDTH, 3 * TOKEN_WIDTH,
                       3 * TOKEN_WIDTH + FOX_HEADS], axis=-1)
            q = rmsnorm(q.reshape(b, s, FOX_HEADS, HEAD_DIM), fox_q_g[j])
            k = rmsnorm(k.reshape(b, s, FOX_HEADS, HEAD_DIM), fox_k_g[j])
            v = v.reshape(b, s, FOX_HEADS, HEAD_DIM)
            log_f = jax.nn.log_sigmoid(f_logit.astype(jnp.float32)
                                       + fox_b_f[j].astype(jnp.float32))
            tok = fox_attention(q, k, v, log_f).reshape(b, s, TOKEN_WIDTH)
        else:
            proj = h @ conv_w_in[j]
            a, gate, cq = jnp.split(proj, [CONV_CHANNELS, 2 * CONV_CHANNELS], axis=-1)
            u = a * jax.nn.sigmoid(gate)
            u = causal_dwconv(u, conv_dw[j]) + conv_dw_b[j]
            u = layernorm(u, conv_ln_g[j], conv_ln_b[j])
            tok = jax.nn.silu(u)
        cq = rmsnorm(cq.reshape(b, s, CROSS_HEADS, HEAD_DIM), cross_q_g[i])
        ck = rmsnorm(k_mem, cross_k_g[i])
        cross = cross_attention(cq, ck, v_mem).reshape(b, s, CROSS_WIDTH)
        x = x + jnp.concatenate([tok, cross], axis=-1) @ mix_w_out[i]

        h = rmsnorm(x, ffn_norm_g[i])
        u = causal_dwconv(h @ ffn_w_up[i], ffn_conv[i])
        ua, ug = jnp.split(u, 2, axis=-1)
        x = x + (jax.nn.silu(ug) * ua) @ ffn_w_down[i]
    return x
```

```python
from contextlib import ExitStack
import numpy as np
import concourse.bass as bass
import concourse.mybir as mybir
from concourse.bass_utils import run_bass_kernel_spmd

F32 = mybir.dt.float32
BF16 = mybir.dt.bfloat16
AF = mybir.ActivationFunctionType
ALU = mybir.AluOpType

ENGS = ("pe", "act", "dve", "pool", "sp")
EPOCH = 12000
N_DMA_SEMS = 24

D = 1024
NCTX = 4096
NOWN = 2112
HALO = 64
OWN0 = NCTX - NOWN
EPS = 1e-6
FFN = 2816
NEG = -30000.0
OB = [(0, 64), (64, 576), (576, 1088), (1088, 1600), (1600, 2112)]


class Op:
    __slots__ = ("eng", "fn", "deps", "sig", "dma", "ticket", "idx", "prev_dma")

    def __init__(self, eng, fn, dma):
        self.eng = eng
        self.fn = fn
        self.dma = dma
        self.deps = []
        self.sig = False
        self.ticket = None
        self.prev_dma = None


class Prog:
    def __init__(self):
        self.ops = {e: [] for e in ENGS}
        self.lastw = {}
        self.readers = {}
        self.n = 0
        self.out_dmas = []

    def add(self, eng, fn, reads=(), writes=(), dma=False, out=False):
        op = Op(eng, fn, dma)
        op.idx = self.n
        self.n += 1
        reads = list(reads) + ["__phase"]
        deps = {}
        for k in reads:
            w = self.lastw.get(k)
            if w is not None:
                deps[w.idx] = (w, True)
        for k in writes:
            w = self.lastw.get(k)
            if w is not None and w.idx not in deps:
                deps[w.idx] = (w, False)
            for r in self.readers.get(k, ()):
                if r.idx not in deps:
                    deps[r.idx] = (r, False)
        for d, raw in deps.values():
            if d is op:
                continue
            if d.eng == op.eng and not d.dma and not op.dma and not raw:
                continue
            if d.eng == "pe" and op.eng == "pe" and not d.dma and not op.dma:
                continue
            op.deps.append(d)
            d.sig = True
        for k in reads:
            self.readers.setdefault(k, []).append(op)
        for k in writes:
            self.lastw[k] = op
            self.readers[k] = []
        self.ops[eng].append(op)
        if out:
            op.sig = True
            self.out_dmas.append(op)
        return op

    def barrier(self, fn):
        op = Op("dve", fn, False)
        op.idx = self.n
        self.n += 1
        seen = set()
        for k, rs in self.readers.items():
            for r in rs:
                if r.idx not in seen:
                    seen.add(r.idx)
                    op.deps.append(r)
                    r.sig = True
        for k, w in self.lastw.items():
            if w.idx not in seen:
                seen.add(w.idx)
                op.deps.append(w)
                w.sig = True
        self.lastw = {"__phase": op}
        self.readers = {}
        self.ops["dve"].append(op)
        return op

    def emit(self, nc, stack):
        sems = {}
        for e in ENGS:
            cnt = 0
            for op in self.ops[e]:
                if op.dma or not op.sig:
                    continue
                ep, c = divmod(cnt, EPOCH)
                op.ticket = ((e, ep), c + 1)
                cnt += 1
            for ep in range((cnt + EPOCH - 1) // EPOCH):
                sems[(e, ep)] = stack.enter_context(nc.semaphore(f"s_{e}{ep}"))
        for j in range(N_DMA_SEMS):
            sems[("dma", j)] = stack.enter_context(nc.semaphore(f"s_dma{j}"))
        all_dma = sorted([op for e in ENGS for op in self.ops[e] if op.dma], key=lambda o: o.idx)
        last_on = [None] * N_DMA_SEMS
        cnt_on = [0] * N_DMA_SEMS
        for i, op in enumerate(all_dma):
            s = i % N_DMA_SEMS
            op.prev_dma = last_on[s]
            cnt_on[s] += 16
            op.ticket = (("dma", s), cnt_on[s])
            last_on[s] = op
        block = stack.enter_context(nc.Block())
        prog = self

        def run(e):
            def body(eng):
                waited = {}

                def wait(t):
                    key, val = t
                    if key[0] != "dma":
                        cur = waited.get(key[0])
                        if cur is not None and (cur[0] > key[1] or (cur[0] == key[1] and cur[1] >= val)):
                            return
                        waited[key[0]] = (key[1], val)
                    else:
                        if waited.get(key, 0) >= val:
                            return
                        waited[key] = val
                    eng.wait_ge(sems[key], val)

                for op in prog.ops[e]:
                    best = {}
                    for d in op.deps:
                        k0 = d.ticket[0]
                        kk = k0 if k0[0] == "dma" else k0[0]
                        cur = best.get(kk)
                        if cur is None or (d.ticket[0][1], d.ticket[1]) > (cur[0][1], cur[1]) or k0[0] == "dma" and d.ticket[1] > cur[1]:
                            best[kk] = d.ticket
                    for t in best.values():
                        wait(t)
                    if op.dma and op.prev_dma is not None:
                        wait(op.prev_dma.ticket)
                    ins = op.fn(eng)
                    if op.dma:
                        ins.then_inc(sems[op.ticket[0]], 16)
                    elif op.sig:
                        ins.then_inc(sems[op.ticket[0]], 1)
                if e == "sp":
                    for op in prog.out_dmas:
                        wait(op.ticket)
            return body

        block.tensor(run("pe"))
        block.scalar(run("act"))
        block.vector(run("dve"))
        block.gpsimd(run("pool"))
        block.sync(run("sp"))


ARENA_WORDS = 52800


class Arena:
    def __init__(self, ap):
        self.ap = ap
        self.base = 0
        self.top = 0
        self.limit = ARENA_WORDS

    def alloc(self, dt, *dims):
        n = int(np.prod(dims))
        words = n if dt == F32 else (n + 1) // 2
        words = (words + 7) // 8 * 8
        off = self.top
        self.top += words
        assert self.top <= self.limit, f"arena overflow {self.top} > {self.limit}"
        v = self.ap[:, off:off + words]
        if dt != F32:
            v = v.bitcast(dt)
        v = v[:, 0:n]
        if len(dims) == 2:
            v = v.rearrange("p (a b) -> p a b", a=dims[0], b=dims[1])
        elif len(dims) == 3:
            v = v.rearrange("p (a b c) -> p a b c", a=dims[0], b=dims[1], c=dims[2])
        return v

    def mark_persistent(self):
        self.base = self.top

    def reset(self):
        self.top = self.base


def build_nc(dbg=None):
    dbg = dbg or {}
    nc = bass.Bass("TRN2", target_bir_lowering=False)

    def din(name, shape):
        return nc.dram_tensor(name, list(shape), F32, kind="ExternalInput").ap()

    xctx = din("xctx", [NCTX, D])
    mem = din("mem", [256, D])
    kmask_d = din("kmask", [128, 32])
    vmask_d = din("vmask", [128, 128])
    vecs_d = din("vecs", [128, 160])
    cw_d = din("cw", [128, 6 * 31])
    fcv_d = din("fcv", [128, 2 * 44 * 3])
    identf_d = din("identf", [128, 128])
    mask_d = din("cmask", [128, 128])
    w_kv = din("mem_w_kv", [D, 512])
    w_out = din("mix_w_out", [2, D, D])
    w_fox = din("fox_w_in", [D, 2572])
    w_cv = din("conv_w_in", [D, 1792])
    w_up = din("ffn_w_up", [2, D, 2 * FFN])
    w_dn = din("ffn_w_down", [2, FFN, D])
    out_d = nc.dram_tensor("out", [2048, D], F32, kind="ExternalOutput").ap()
    dbg_out = {}
    for name, shape in dbg.items():
        dbg_out[name] = nc.dram_tensor("dbg_" + name, list(shape), F32, kind="ExternalOutput").ap()

    st = ExitStack()
    with st:
        arena_t = st.enter_context(nc.sbuf_tensor("arena", [128, ARENA_WORDS], F32))
        A = Arena(arena_t[:])
        class PS:
            def __init__(self, t, name):
                self.t = t
                self.name = name

            def __getitem__(self, idx):
                return self.t[idx]
        ps = [PS(st.enter_context(nc.psum_tensor(f"ps{i}", [128, 512], F32)), f"ps{i}") for i in range(8)]
        P = Prog()
        rot = {"i": 0}

        def nextps(lst):
            rot["i"] += 1
            return lst[rot["i"] % len(lst)]

        identf = A.alloc(F32, 128)
        identb = A.alloc(BF16, 128)
        onesb = A.alloc(BF16, 128)
        bdiagb = A.alloc(BF16, 128)
        cmaskb = A.alloc(BF16, 128)
        onesf = A.alloc(F32, 512)
        vecs = A.alloc(F32, 160)
        cw = A.alloc(F32, 6, 31)
        fcv = A.alloc(F32, 2 * 44, 3)
        kmask = A.alloc(F32, 32)
        vmaskb = A.alloc(BF16, 128)
        negbf = A.alloc(F32, 1)
        V_MIXG, V_FFNG, V_MEMG = 0, 16, 32
        V_CQG, V_CKG, V_FQG, V_FKG = 40, 42, 44, 45
        V_CVB, V_LNG, V_LNB, V_FB = 46, 52, 58, 64

        P.add("sp", lambda e: e.dma_start(out=identf, in_=identf_d), writes=["identf"], dma=True)
        P.add("sp", lambda e: e.dma_start(out=vecs, in_=vecs_d), writes=["vecs"], dma=True)
        P.add("sp", lambda e: e.dma_start(out=cw.rearrange("p a b -> p (a b)"), in_=cw_d), writes=["cw"], dma=True)
        P.add("sp", lambda e: e.dma_start(out=fcv.rearrange("p a b -> p (a b)"), in_=fcv_d), writes=["fcv"], dma=True)
        P.add("sp", lambda e: e.dma_start(out=kmask, in_=kmask_d), writes=["kmask"], dma=True)
        P.add("pool", lambda e: e.dma_start(out=cmaskb, in_=mask_d), writes=["cmaskb"], dma=True)
        P.add("pool", lambda e: e.dma_start(out=vmaskb, in_=vmask_d), writes=["vmaskb"], dma=True)
        P.add("pool", lambda e: e.dma_start(out=identb, in_=identf_d), writes=["identb"], dma=True)
        P.add("dve", lambda e: e.memset(onesb, 1.0), writes=["onesb"])
        P.add("dve", lambda e: e.memset(onesf, 1.0), writes=["onesf"])
        P.add("dve", lambda e: e.memset(bdiagb, 0.0), writes=["bdiagb"])

        def bd2(e):
            e.memset(bdiagb[0:64, 0:64], 1.0)
            return e.memset(bdiagb[64:128, 64:128], 1.0)
        P.add("dve", bd2, writes=["bdiagb"])
        P.add("dve", lambda e: e.tensor_scalar(out=negbf[0:12, :], in0=vecs[0:12, V_FB:V_FB + 1], scalar1=-1.0,
                                               scalar2=0.0, op0=ALU.mult, op1=ALU.add), reads=["vecs"], writes=["negbf"])

        XT_OFF = ARENA_WORDS - 8 * NOWN
        xT = A.ap[:, XT_OFF:ARENA_WORDS].rearrange("p (a b) -> p a b", a=8, b=NOWN)
        ckT = A.alloc(BF16, 2, 2, 256)
        vmem = A.alloc(BF16, 2, 4, 128)
        A.mark_persistent()

        def dump(name, src_ap, reads):
            if name in dbg_out:
                P.add("pool", lambda e: e.dma_start(out=dbg_out[name], in_=src_ap), reads=reads, dma=True, out=True)

        def wload(dst, src, key, eng="pool"):
            P.add(eng, lambda e: e.dma_start(out=dst, in_=src), writes=[key], dma=True)

        def wsrc(w, c0, c1):
            return w.rearrange("(k p) n -> p k n", p=128)[:, :, c0:c1]

        def rstd_from(ps_t, rows, n, inv_n, out_ap, rkey, wkey):
            P.add("act", lambda e: e.activation(out=out_ap, in_=ps_t[rows, 0:n], func=AF.Sqrt, scale=inv_n, bias=epsb[rows, :]),
                  reads=[rkey, "epsb"], writes=[wkey])
            P.add("dve", lambda e: e.reciprocal(out=out_ap, in_=out_ap), reads=[wkey], writes=[wkey])

        epsb = A.alloc(F32, 1)
        nhalf = A.alloc(F32, 512)
        A.mark_persistent()
        P.add("dve", lambda e: e.memset(epsb, EPS), writes=["epsb"])
        P.add("pool", lambda e: e.memset(nhalf, -0.5), writes=["nhalf"])

        def rstd_ps(src, inv_n, out_ap, rkeys, wkey, evac):
            np_ = out_ap.shape[0]
            if evac == "dve":
                P.add("act", lambda e: e.activation(out=out_ap, in_=src, func=AF.Ln, scale=inv_n, bias=epsb[0:np_, :]),
                      reads=rkeys + ["epsb"], writes=[wkey])
                P.add("act", lambda e: e.activation(out=out_ap, in_=out_ap, func=AF.Exp, scale=-0.5), reads=[wkey], writes=[wkey])
            else:
                P.add("act", lambda e: e.activation(out=out_ap, in_=src, func=AF.Sqrt, scale=inv_n, bias=epsb[0:np_, :]),
                      reads=rkeys + ["epsb"], writes=[wkey])
                P.add("dve", lambda e: e.reciprocal(out=out_ap, in_=out_ap), reads=[wkey], writes=[wkey])

        PSG = [ps[6], ps[7]]
        memx = A.alloc(F32, 2, D)
        memT = A.alloc(F32, 8, 256)
        msq = A.alloc(BF16, 8, 256)
        mrs = A.alloc(F32, 256)
        hmT = A.alloc(BF16, 8, 256)
        wkv = A.alloc(BF16, 8, 512)
        kraw = A.alloc(F32, 2, 256)
        ksq = A.alloc(BF16, 256)
        krs = A.alloc(F32, 256)
        wload(wkv, wsrc(w_kv, 0, 512), "wkv")
        P.add("sp", lambda e: e.dma_start(out=memx, in_=mem.rearrange("(t p) d -> p t d", p=128)), writes=["memx"], dma=True)
        for t in range(2):
            for k in range(8):
                pt = nextps(ps)
                P.add("pe", lambda e, pt=pt, t=t, k=k: e.transpose(pt[:, 0:128], memx[:, t, k * 128:(k + 1) * 128], identf),
                      reads=["memx", "identf"], writes=[pt.name])
                P.add("act", lambda e, pt=pt, t=t, k=k: e.copy(out=memT[:, k, t * 128:(t + 1) * 128], in_=pt[:, 0:128]),
                      reads=[pt.name], writes=[f"memT{k}"])
        pt = nextps(ps)
        for k in range(8):
            P.add("act", lambda e, k=k: e.activation(out=msq[:, k, :], in_=memT[:, k, :], func=AF.Square),
                  reads=[f"memT{k}"], writes=[f"msq{k}"])
            P.add("pe", lambda e, k=k, pt=pt: e.matmul(pt[:, 0:256], lhsT=onesb, rhs=msq[:, k, :], start=(k == 0), stop=(k == 7)),
                  reads=[f"msq{k}", "onesb"], writes=[pt.name])
        rstd_from(pt, slice(0, 128), 256, 1.0 / D, mrs, pt.name, "mrs")
        for k in range(8):
            P.add("dve", lambda e, k=k: e.scalar_tensor_tensor(out=hmT[:, k, :], in0=memT[:, k, :], scalar=vecs[:, V_MEMG + k:V_MEMG + k + 1],
                                                              in1=mrs, op0=ALU.mult, op1=ALU.mult),
                  reads=[f"memT{k}", "mrs", "vecs"], writes=[f"hmT{k}"])
        for c in range(2):
            pt = nextps(ps)
            for k in range(8):
                P.add("pe", lambda e, k=k, c=c, pt=pt: e.matmul(pt[:, 0:256], lhsT=wkv[:, k, c * 128:(c + 1) * 128], rhs=hmT[:, k, :],
                                                               start=(k == 0), stop=(k == 7)),
                      reads=[f"hmT{k}", "wkv"], writes=[pt.name])
            P.add("act", lambda e, c=c, pt=pt: e.copy(out=kraw[:, c, :], in_=pt[:, 0:256]), reads=[pt.name], writes=[f"kraw{c}"])
            P.add("act", lambda e, c=c: e.activation(out=ksq, in_=kraw[:, c, :], func=AF.Square), reads=[f"kraw{c}"], writes=["ksq"])
            pt2 = nextps(ps)
            P.add("pe", lambda e, pt2=pt2: e.matmul(pt2[:, 0:256], lhsT=bdiagb, rhs=ksq, start=True, stop=True),
                  reads=["ksq", "bdiagb"], writes=[pt2.name])
            rstd_from(pt2, slice(0, 128), 256, 1.0 / 64, krs, pt2.name, "krs")
            for l in range(2):
                P.add("dve", lambda e, c=c, l=l: e.scalar_tensor_tensor(out=ckT[:, l, c, :], in0=kraw[:, c, :],
                                                                      scalar=vecs[:, V_CKG + l:V_CKG + l + 1], in1=krs,
                                                                      op0=ALU.mult, op1=ALU.mult),
                      reads=[f"kraw{c}", "krs", "vecs"], writes=[f"ckT{l}{c}"])
        P.add("dve", lambda e: e.memset(vmem.rearrange("p a b c -> p (a b c)"), 1.0), writes=["vmem"])
        for t in range(2):
            pt = nextps(ps)
            for k in range(8):
                P.add("pe", lambda e, k=k, t=t, pt=pt: e.matmul(pt[:, 0:256], lhsT=hmT[:, k, t * 128:(t + 1) * 128], rhs=wkv[:, k, 256:512],
                                                               start=(k == 0), stop=(k == 7)),
                      reads=[f"hmT{k}", "wkv"], writes=[pt.name])
            for h in range(4):
                lo = 0 if h % 2 == 0 else 64
                P.add("act", lambda e, t=t, h=h, lo=lo, pt=pt: e.copy(out=vmem[:, t, h, lo:lo + 64], in_=pt[:, h * 64:(h + 1) * 64]),
                      reads=[pt.name], writes=["vmem"])

        P.barrier(lambda e: e.memset(onesf[:, 0:8], 1.0))
        A.reset()
        hT = A.alloc(BF16, 8, NCTX)
        btab = A.alloc(F32, 12, 32, 9)
        mark1 = A.top
        xs = [A.alloc(F32, D) for _ in range(3)]
        junk = A.alloc(BF16, D)
        hn = [A.alloc(BF16, D) for _ in range(2)]
        ssq = A.alloc(F32, 32)
        rsd = A.alloc(F32, 32)
        wf = A.alloc(BF16, 8, 12)
        fe = A.alloc(F32, 512)
        cc = A.alloc(F32, NCTX)
        cT = A.alloc(F32, 32, 12)
        crefb = A.alloc(F32, 9, 12)
        tmpb = A.alloc(F32, 32)
        wload(wf, wsrc(w_fox, 2304, 2316), "wf")
        gmix0 = vecs[:, V_MIXG:V_MIXG + 8]
        p1pend = []
        for t in range(32):
            xb_ = xs[t % 3]
            hb_ = hn[t % 2]
            pt = nextps(ps[0:4])
            ptb = pt[:].bitcast(BF16).rearrange("p (a b) -> p a b", a=8, b=128)
            P.add("sp", lambda e, t=t, xb_=xb_: e.dma_start(out=xb_, in_=xctx[t * 128:(t + 1) * 128, :]), writes=[f"xs{t % 3}"], dma=True)
            P.add("act", lambda e, t=t, xb_=xb_: e.activation(out=junk, in_=xb_, func=AF.Square, accum_out=ssq[:, t:t + 1]),
                  reads=[f"xs{t % 3}"], writes=["junk", f"ssq{t}"])
            P.add("act", lambda e, t=t: e.activation(out=rsd[:, t:t + 1], in_=ssq[:, t:t + 1], func=AF.Sqrt, scale=1.0 / D, bias=epsb),
                  reads=[f"ssq{t}", "epsb"], writes=[f"rsd{t}"])
            P.add("dve", lambda e, t=t: e.reciprocal(out=rsd[:, t:t + 1], in_=rsd[:, t:t + 1]), reads=[f"rsd{t}"], writes=[f"rsd{t}"])
            P.add("dve", lambda e, t=t, xb_=xb_, hb_=hb_: e.tensor_scalar(out=hb_, in0=xb_, scalar1=rsd[:, t:t + 1], scalar2=0.0, op0=ALU.mult, op1=ALU.add),
                  reads=[f"xs{t % 3}", f"rsd{t}"], writes=[f"hn{t % 2}"])

            def tr(e, hb_=hb_, ptb=ptb):
                for k in range(8):
                    r = e.transpose(ptb[:, k, :], hb_[:, k * 128:(k + 1) * 128], identb)
                return r
            P.add("pe", tr, reads=[f"hn{t % 2}", "identb"], writes=[pt.name])
            if p1pend:
                p1pend.pop()()

            def evac(t=t, ptb=ptb, pt=pt):
                P.add("dve", lambda e: e.tensor_tensor(out=hT[:, :, t * 128:(t + 1) * 128], in0=ptb,
                                                       in1=gmix0.unsqueeze(2).to_broadcast([128, 8, 128]), op=ALU.mult),
                      reads=[pt.name, "vecs"], writes=[f"hT{t}"])
            p1pend.append(evac)
        p1pend.pop()()
        for cb in range(8):
            pt = nextps(PSG)

            def fmm(e, cb=cb, pt=pt):
                for k in range(8):
                    r = e.matmul(pt[0:12, :], lhsT=wf[:, k, :], rhs=hT[:, k, cb * 512:(cb + 1) * 512], start=(k == 0), stop=(k == 7))
                return r
            P.add("pe", fmm, reads=[f"hT{4 * cb + i}" for i in range(4)] + ["wf"], writes=[pt.name])
            P.add("act", lambda e, pt=pt: e.activation(out=fe[0:12, :], in_=pt[0:12, :], func=AF.Exp, scale=-1.0, bias=negbf[0:12, :]),
                  reads=[pt.name, "negbf"], writes=["fe"])
            P.add("act", lambda e: e.activation(out=fe[0:12, :], in_=fe[0:12, :], func=AF.Ln, bias=onesf[0:12, 0:1]), reads=["fe", "onesf"], writes=["fe"])
            init = 0.0 if cb == 0 else cc[0:12, cb * 512 - 1:cb * 512]
            P.add("dve", lambda e, cb=cb, init=init: e.tensor_tensor_scan(out=cc[0:12, cb * 512:(cb + 1) * 512], data0=onesf[0:12, :],
                                                                          data1=fe[0:12, :], initial=init, op0=ALU.mult, op1=ALU.subtract),
                  reads=["fe", "onesf", "cc"], writes=["cc"])
        pt = nextps(PSG)
        ptv = pt[:, 0:384].rearrange("p (a b) -> p a b", a=32, b=12)

        def ctr(e, ptv=ptv):
            for t in range(32):
                r = e.transpose(ptv[:, t, :], cc[0:12, t * 128:(t + 1) * 128], identf[0:12, 0:12])
            return r
        P.add("pe", ctr, reads=["cc", "identf"], writes=[pt.name])
        P.add("act", lambda e, ptv=ptv: e.copy(out=cT, in_=ptv), reads=[pt.name], writes=["cT"])
        pt2 = nextps(PSG)
        P.add("pe", lambda e, pt2=pt2: e.matmul(pt2[:, 0:108].rearrange("p (a b) -> p a b", a=9, b=12), lhsT=onesf[0:1, 0:128],
                                                rhs=cT[0:1, 15:32:2, :], start=True, stop=True),
              reads=["cT", "onesf"], writes=[pt2.name])
        P.add("act", lambda e, pt2=pt2: e.copy(out=crefb, in_=pt2[:, 0:108].rearrange("p (a b) -> p a b", a=9, b=12)),
              reads=[pt2.name], writes=["crefb"])
        for h in range(12):
            P.add("dve", lambda e, h=h: e.tensor_tensor(out=tmpb, in0=kmask, in1=cT[:, :, h], op=ALU.subtract),
                  reads=["kmask", "cT"], writes=["tmpb"])
            for sb in range(9):
                P.add("dve", lambda e, h=h, sb=sb: e.tensor_scalar(out=btab[:, h, :, sb], in0=tmpb, scalar1=crefb[:, sb, h:h + 1],
                                                                  scalar2=0.0, op0=ALU.add, op1=ALU.add),
                      reads=["tmpb", "crefb"], writes=["btab"])
        dump("hT", hT[:, 0, :], [f"hT{t}" for t in range(32)])
        dump("cc", cc[0:12, :], ["cc"])

        P.barrier(lambda e: e.memset(onesf[:, 0:8], 1.0))
        A.top = mark1
        attnT0 = A.alloc(BF16, 8, NOWN)
        mark_att = A.top
        SB_OF = [(0, HALO)] + [(HALO + 256 * i, HALO + 256 * (i + 1)) for i in range(8)]

        def qknorm_stages(pt_raw, n, gcol, out_ap, okey, sqb, rsb, tag, banks=None):
            p2 = nextps(banks or PSG)

            def s1():
                P.add("act", lambda e: e.activation(out=sqb[:, 0:n], in_=pt_raw[:, 0:n], func=AF.Square), reads=[pt_raw.name], writes=[tag + "sq"])

            def s2():
                P.add("pe", lambda e: e.matmul(p2[:, 0:n], lhsT=bdiagb, rhs=sqb[:, 0:n], start=True, stop=True),
                      reads=[tag + "sq", "bdiagb"], writes=[p2.name])

            def s3():
                rstd_ps(p2[:, 0:n], 1.0 / 64, rsb[:, 0:n], [p2.name], tag + "rs", "dve")

            def s4():
                P.add("dve", lambda e: e.scalar_tensor_tensor(out=out_ap, in0=pt_raw[:, 0:n], scalar=gcol, in1=rsb[:, 0:n], op0=ALU.mult, op1=ALU.mult),
                      reads=[pt_raw.name, tag + "rs", "vecs"], writes=[okey])
            return [s1, s2, s3, s4]

        def qknorm(pt_raw, n, gcol, out_ap, okey, sqb, rsb, tag):
            for f_ in qknorm_stages(pt_raw, n, gcol, out_ap, okey, sqb, rsb, tag):
                f_()

        def proj_half(pt, n, wsl, rhs_fn, rkeys, half):
            def f(e):
                for k in range(4 * half, 4 * half + 4):
                    r = e.matmul(pt[:, 0:n], lhsT=wsl(k), rhs=rhs_fn(k), start=(k == 0), stop=(k == 7))
                return r
            P.add("pe", f, reads=rkeys, writes=[pt.name])

        def proj(pt, n, wsl, rhs_fn, rkeys):
            def f(e):
                for k in range(8):
                    r = e.matmul(pt[:, 0:n], lhsT=wsl(k), rhs=rhs_fn(k), start=(k == 0), stop=(k == 7))
                return r
            P.add("pe", f, reads=rkeys, writes=[pt.name])

        def attention(qT, qkey, n_kt, kT_fn, kkey, v_fn, vkey, bias_fn, causal, dst_fn, dkey, pbufs, rcp, tag, fillers=(), every=5, obanks=None):
            SP = [[ps[0], ps[1]], [ps[2], ps[3]]]
            obanks = obanks or [(ps[4], ps[5])]

            def qk_f(e, kt, c0, n, b0, b1, sA, sB):
                kk = kT_fn(kt)
                e.matmul(sA[:, c0:n], lhsT=kk[0:64, :], rhs=qT[0:64, b0 + c0:b1], start=True, stop=True)
                return e.matmul(sB[:, c0:n], lhsT=kk[64:128, :], rhs=qT[64:128, b0 + c0:b1], start=True, stop=True)

            def ex_f(e, s_, p_, lo, hi, bias):
                return e.activation(out=p_[:, lo:hi], in_=s_[:, lo:hi], func=AF.Exp, scale=0.125, bias=bias)

            def mk_f(e, p_, m0, w):
                return e.tensor_tensor(out=p_[:, m0:m0 + w], in0=p_[:, m0:m0 + w], in1=cmaskb[:, 128 - w:128], op=ALU.mult)

            def pv_f(e, kt, c0, n, pA, pB, first, last, OA, OBk):
                vA, vB = v_fn(kt)
                e.matmul(OA[:, c0:n], lhsT=vA, rhs=pA[:, c0:n], start=first, stop=last)
                return e.matmul(OBk[:, c0:n], lhsT=vB, rhs=pB[:, c0:n], start=first, stop=last)

            def bind(f, *a):
                return lambda e: f(e, *a)

            fillers = list(fillers)
            nfill = len(fillers)
            popped = [0]
            step = [0]
            tot_steps = 0
            for bi_, (b0_, b1_) in enumerate(OB):
                qt0_ = 15 if bi_ == 0 else 16 + 4 * (bi_ - 1)
                tot_steps += (qt0_ + max(1, (b1_ - b0_) // 128)) if causal else n_kt
            tot_steps = max(1, tot_steps - 4)

            for bi, (b0, b1) in enumerate(OB):
                n = b1 - b0
                OA, OBk = obanks[bi % len(obanks)]
                qt0 = 15 if bi == 0 else 16 + 4 * (bi - 1)
                nt = max(1, n // 128)
                last_kt = (qt0 + nt - 1) if causal else n_kt - 1
                kts = list(range(last_kt + 1))
                sbs = [0] if bi == 0 else [2 * bi - 1, 2 * bi]

                def cols_for(kt, qt0=qt0):
                    if not causal:
                        return 0
                    return max(0, kt - qt0) * 128

                def qk(kt, i):
                    sA, sB = SP[i % 2]
                    P.add("pe", bind(qk_f, kt, cols_for(kt), n, b0, b1, sA, sB), reads=[qkey, kkey], writes=[sA.name, sB.name])

                def ex(kt, i):
                    c0 = cols_for(kt)
                    sA, sB = SP[i % 2]
                    pA, pB = pbufs[i % 3]
                    for hh, (s_, p_) in enumerate(((sA, pA), (sB, pB))):
                        pk = f"{tag}p{i % 3}{hh}"
                        for sb in sbs:
                            o0, o1 = SB_OF[sb]
                            lo = max(o0 - b0, c0)
                            hi = o1 - b0
                            if lo >= hi:
                                continue
                            P.add("act", bind(ex_f, s_, p_, lo, hi, bias_fn(hh, kt, sb)), reads=[s_.name, "btab", "zb"], writes=[pk])
                        if causal and kt >= qt0:
                            P.add("pool", bind(mk_f, p_, (kt - qt0) * 128, min(128, n)), reads=[pk, "cmaskb"], writes=[pk])

                def pv(kt, i):
                    pA, pB = pbufs[i % 3]
                    P.add("pe", bind(pv_f, kt, cols_for(kt), n, pA, pB, kt == 0, kt == last_kt, OA, OBk),
                          reads=[f"{tag}p{i % 3}0", f"{tag}p{i % 3}1", vkey], writes=[OA.name, OBk.name])

                qk(kts[0], 0)
                if len(kts) > 1:
                    qk(kts[1], 1)
                for i, kt in enumerate(kts):
                    ex(kt, i)
                    if i + 2 < len(kts):
                        qk(kts[i + 2], i + 2)
                    pv(kt, i)
                    step[0] += 1
                    want = (step[0] * nfill) // tot_steps
                    while fillers and popped[0] < want:
                        f_ = fillers.pop(0)
                        if f_ is not None:
                            f_()
                        popped[0] += 1
                rall = rcp[:, 0:n]
                rk = tag + "rcp"
                P.add("dve", bind(lambda e, o, i_: e.tensor_copy(out=o, in_=i_), rcp[0:64, 0:n], OBk[0:64, 0:n]), reads=[OBk.name], writes=[rk])
                P.add("dve", bind(lambda e, o, i_: e.tensor_copy(out=o, in_=i_), rcp[64:128, 0:n], OA[64:128, 0:n]), reads=[OA.name], writes=[rk])
                if bi == 0:
                    P.add("dve", bind(lambda e, o: e.tensor_scalar(out=o, in0=o, scalar1=1e-30, scalar2=0.0, op0=ALU.max, op1=ALU.add), rall),
                          reads=[rk], writes=[rk])
                    P.add("dve", bind(lambda e, o: e.reciprocal(out=o, in_=o), rall), reads=[rk], writes=[rk])
                else:
                    P.add("dve", bind(lambda e, o: e.reciprocal(out=o, in_=o), rall), reads=[rk], writes=[rk])
                P.add("dve", bind(lambda e, o, a_, r_: e.tensor_tensor(out=o, in0=a_, in1=r_, op=ALU.mult), dst_fn(slice(0, 64), b0, b1), OA[0:64, 0:n], rcp[64:128, 0:n]),
                      reads=[OA.name, rk], writes=[dkey])
                P.add("dve", bind(lambda e, o, a_, r_: e.tensor_tensor(out=o, in0=a_, in1=r_, op=ALU.mult), dst_fn(slice(64, 128), b0, b1), OBk[64:128, 0:n], rcp[0:64, 0:n]),
                      reads=[OBk.name, rk], writes=[dkey])

            for f_ in fillers:
                if f_ is not None:
                    f_()

        rcp0 = A.alloc(F32, 512)
        pbufs0 = [[A.alloc(BF16, 512), A.alloc(BF16, 512)] for _ in range(3)]
        sqb0 = A.alloc(BF16, 512)
        rsb0 = A.alloc(F32, 512)
        mark2 = A.top
        kTp = [A.alloc(BF16, NCTX) for _ in range(2)]
        qTp0 = [A.alloc(BF16, NOWN) for _ in range(2)]
        Vp = [A.alloc(BF16, 32, 192) for _ in range(2)]
        wp0 = [A.alloc(BF16, 8, 384) for _ in range(2)]
        for i in range(2):
            P.add("pool", lambda e, i=i: e.memset(Vp[i][:, :, 64:128], 1.0), writes=[f"Vp{i}"])
        hT_keys = [f"hT{t}" for t in range(32)]

        def fox_pair_work(p, banks=None):
            i = p % 2
            banks = banks or PSG

            def loads():
                wload(wp0[i][:, :, 0:128], wsrc(w_fox, p * 128, (p + 1) * 128), f"wp{i}q")
                wload(wp0[i][:, :, 128:256], wsrc(w_fox, 768 + p * 128, 768 + (p + 1) * 128), f"wp{i}k")
                wload(wp0[i][:, :, 256:384], wsrc(w_fox, 1536 + p * 128, 1536 + (p + 1) * 128), f"wp{i}v")
            chunks = []

            def kc(cb):
                pt = nextps(banks)
                st_ = qknorm_stages(pt, 512, vecs[:, V_FKG:V_FKG + 1], kTp[i][:, cb * 512:(cb + 1) * 512], f"kTp{i}", sqb0, rsb0, "n0", banks=banks)

                def s1a():
                    proj_half(pt, 512, lambda k: wp0[i][:, k, 128:256], lambda k: hT[:, k, cb * 512:(cb + 1) * 512],
                              hT_keys[4 * cb:4 * cb + 4] + [f"wp{i}k"], 0)

                def s1b():
                    proj_half(pt, 512, lambda k: wp0[i][:, k, 128:256], lambda k: hT[:, k, cb * 512:(cb + 1) * 512],
                              hT_keys[4 * cb:4 * cb + 4] + [f"wp{i}k"], 1)
                return [s1a, s1b, None, st_[0], st_[1], None, st_[2], None, st_[3]]

            def qc(b0, b1):
                pt = nextps(banks)
                n = b1 - b0
                st_ = qknorm_stages(pt, n, vecs[:, V_FQG:V_FQG + 1], qTp0[i][:, b0:b1], f"qTp{i}", sqb0, rsb0, "n0", banks=banks)

                def s1a():
                    proj_half(pt, n, lambda k: wp0[i][:, k, 0:128], lambda k: hT[:, k, OWN0 + b0:OWN0 + b1],
                              hT_keys[(OWN0 + b0) // 128:(OWN0 + b1) // 128] + [f"wp{i}q"], 0)

                def s1b():
                    proj_half(pt, n, lambda k: wp0[i][:, k, 0:128], lambda k: hT[:, k, OWN0 + b0:OWN0 + b1],
                              hT_keys[(OWN0 + b0) // 128:(OWN0 + b1) // 128] + [f"wp{i}q"], 1)
                return [s1a, s1b, None, st_[0], st_[1], None, st_[2], None, st_[3]]

            def vc(t2):
                pt = nextps(banks)

                def vmm(e, j):
                    t = 2 * t2 + j
                    for k in range(8):
                        r = e.matmul(pt[:, j * 128:(j + 1) * 128], lhsT=hT[:, k, t * 128:(t + 1) * 128], rhs=wp0[i][:, k, 256:384],
                                     start=(k == 0), stop=(k == 7))
                    return r

                def s1():
                    P.add("pe", lambda e: vmm(e, 0), reads=hT_keys[2 * t2:2 * t2 + 2] + [f"wp{i}v"], writes=[pt.name])

                def s1b():
                    P.add("pe", lambda e: vmm(e, 1), reads=hT_keys[2 * t2:2 * t2 + 2] + [f"wp{i}v"], writes=[pt.name])
                ptv = pt[:, 0:256].rearrange("p (a b) -> p a b", a=2, b=128)

                def s2():
                    P.add("dve", lambda e: e.tensor_copy(out=Vp[i][:, 2 * t2:2 * t2 + 2, 0:64], in_=ptv[:, :, 0:64]),
                          reads=[pt.name], writes=[f"Vp{i}"])
                    P.add("dve", lambda e: e.tensor_copy(out=Vp[i][:, 2 * t2:2 * t2 + 2, 128:192], in_=ptv[:, :, 64:128]),
                          reads=[pt.name], writes=[f"Vp{i}"])
                return [s1, s1b, None, s2]
            for cb in range(8):
                chunks.extend(kc(cb))
            for (b0, b1) in OB:
                chunks.extend(qc(b0, b1))
            for t2 in range(16):
                chunks.extend(vc(t2))
            return loads, chunks

        zb = A.alloc(F32, 1)
        P.add("dve", lambda e: e.memset(zb, 0.0), writes=["zb"])

        def cross_work(layer, c, w_src, col0, hsrc_fn, hkeys_fn, wpX, qTpX, sqbX, rsbX, banks=None):
            i = c % 2
            banks = banks or PSG

            def loads():
                wload(wpX[i][:, :, 0:128], wsrc(w_src, col0 + c * 128, col0 + (c + 1) * 128), f"wp{i}q")
            chunks = []

            def qc(b0, b1):
                pt = nextps(banks)
                n = b1 - b0
                st_ = qknorm_stages(pt, n, vecs[:, V_CQG + layer:V_CQG + layer + 1], qTpX[i][:, b0:b1], f"qTp{i}", sqbX, rsbX, "n0" if sqbX is sqb0 else "n1", banks=banks)

                def s1():
                    proj(pt, n, lambda k: wpX[i][:, k, 0:128], lambda k: hsrc_fn(k, b0, b1), hkeys_fn(b0, b1) + [f"wp{i}q"])
                return [s1] + st_
            for (b0, b1) in OB:
                chunks.extend(qc(b0, b1))
            return loads, chunks

        def cross_att(layer, c, qTpX, dst, pbX, rcX, zbX, tag, fillers=(), every=2, obanks=None):
            i = c % 2
            attention(qTpX[i], f"qTp{i}", 2, lambda kt: ckT[:, layer, c, kt * 128:(kt + 1) * 128], f"ckT{layer}{c}",
                      lambda kt: (vmem[:, kt, 2 * c, :], vmem[:, kt, 2 * c + 1, :]), "vmem",
                      lambda hh, kt, sb: zbX, False,
                      lambda rows, b0, b1: dst[rows, c, b0:b1], f"attnT{6 + c}", pbX, rcX, tag, fillers=fillers, every=every, obanks=obanks)

        h0src = lambda k, b0, b1: hT[:, k, OWN0 + b0:OWN0 + b1]
        h0keys = lambda b0, b1: hT_keys[(OWN0 + b0) // 128:(OWN0 + b1) // 128]
        attn67 = attnT0[:, 6:8, :]
        l0, c0_ = fox_pair_work(0, banks=list(ps))
        l0()
        for f_ in c0_:
            if f_ is not None:
                f_()
        for p in range(6):
            i = p % 2
            if p < 5:
                nl, nch = fox_pair_work(p + 1)
            else:
                nl, nch = cross_work(0, 0, w_fox, 2316, h0src, h0keys, wp0, qTp0, sqb0, rsb0)
            nl()
            attention(qTp0[i], f"qTp{i}", 32, lambda kt, i=i: kTp[i][:, kt * 128:(kt + 1) * 128], f"kTp{i}",
                      lambda kt, i=i: (Vp[i][:, kt, 0:128], Vp[i][:, kt, 64:192]), f"Vp{i}",
                      lambda hh, kt, sb, p=p: btab[:, 2 * p + hh, kt, sb:sb + 1], True,
                      lambda rows, b0, b1, p=p: attnT0[rows, p, b0:b1], f"attnT{p}", pbufs0, rcp0, "b0", fillers=nch, every=2)
        nl, nch = cross_work(0, 1, w_fox, 2316, h0src, h0keys, wp0, qTp0, sqb0, rsb0)
        nl()
        cross_att(0, 0, qTp0, attn67, pbufs0, rcp0, zb, "b0", fillers=nch, every=1)
        cross_att(0, 1, qTp0, attn67, pbufs0, rcp0, zb, "b0", obanks=[(ps[4], ps[5]), (ps[6], ps[7])])
        dump("attnT0", attnT0[:, 0, :], ["attnT0"])
        dump("attnT6", attnT0[:, 6, :], ["attnT6"])

        attn_keys = [f"attnT{c}" for c in range(8)]

        def outproj_block(wo, src_fn, src_keys, bi):
            b0, b1 = OB[bi]
            n = b1 - b0
            for dc in range(8):
                pt = nextps(PSALL)
                proj(pt, n, lambda k, dc=dc: wo[:, k, dc * 128:(dc + 1) * 128], lambda k: src_fn(k, b0, b1), src_keys + [f"wo{dc // 4}"])
                P.add("dve", lambda e, pt=pt, dc=dc: e.tensor_tensor(out=xT[:, dc, b0:b1], in0=xT[:, dc, b0:b1], in1=pt[:, 0:n], op=ALU.add),
                      reads=[pt.name, f"xT{dc}"], writes=[f"xT{dc}"])

        def outproj(layer, wo, src_fn, src_keys):
            for half in range(2):
                wload(wo[:, :, half * 512:(half + 1) * 512], wsrc(w_out[layer], half * 512, (half + 1) * 512), f"wo{half}")
            for dc in range(8):
                for (b0, b1) in OB:
                    pt = nextps(PSALL)
                    n = b1 - b0
                    proj(pt, n, lambda k, dc=dc: wo[:, k, dc * 128:(dc + 1) * 128], lambda k, b0=b0, b1=b1: src_fn(k, b0, b1),
                         src_keys + [f"wo{dc // 4}"])
                    P.add("dve", lambda e, pt=pt, dc=dc, b0=b0, b1=b1, n=n: e.tensor_tensor(out=xT[:, dc, b0:b1], in0=xT[:, dc, b0:b1], in1=pt[:, 0:n], op=ALU.add),
                          reads=[pt.name, f"xT{dc}"], writes=[f"xT{dc}"])

        P.barrier(lambda e: e.memset(onesf[:, 0:8], 1.0))
        A.reset()
        A.limit = XT_OFF
        assert A.top + 8000 < mark1
        PSALL = ps
        wo = A.alloc(BF16, 8, D)
        xs2 = [A.alloc(F32, D) for _ in range(3)]
        for t in range(17):
            xb_ = xs2[t % 3]
            nr = min(128, NOWN - t * 128)
            P.add("sp", lambda e, t=t, xb_=xb_, nr=nr: e.dma_start(out=xb_[0:nr, :], in_=xctx[OWN0 + t * 128:OWN0 + t * 128 + nr, :]), writes=[f"xs2{t % 3}"], dma=True)
            for k2 in range(2):
                pt = nextps(PSALL)

                def tr2(e, xb_=xb_, pt=pt, k2=k2, nr=nr):
                    for j in range(4):
                        k = 4 * k2 + j
                        r = e.transpose(pt[:, j * 128:j * 128 + nr], xb_[0:nr, k * 128:(k + 1) * 128], identf[0:nr, 0:nr])
                    return r
                P.add("pe", tr2, reads=[f"xs2{t % 3}", "identf"], writes=[pt.name])
                src = pt[:, :].rearrange("p (a b) -> p a b", a=4, b=128)[:, :, 0:nr]
                dst = xT[:, 4 * k2:4 * k2 + 4, t * 128:t * 128 + nr]
                if k2 == 0:
                    P.add("act", lambda e, src=src, dst=dst: e.copy(out=dst, in_=src),
                          reads=[pt.name], writes=[f"xT{4 * k2 + j}" for j in range(4)])
                else:
                    P.add("dve", lambda e, src=src, dst=dst: e.tensor_copy(out=dst, in_=src),
                          reads=[pt.name], writes=[f"xT{4 * k2 + j}" for j in range(4)])
        outproj(0, wo, lambda k, b0, b1: attnT0[:, k, b0:b1], attn_keys)
        dump("x0p", xT[:, 0, :], ["xT0"])

        def rms_feat(gcol0, hdst, hkey, tmp_sq, tmp_rs):
            pts = {}

            def stats(bi):
                b0, b1 = OB[bi]
                n = b1 - b0
                pt = nextps(PSALL)
                pts[bi] = pt
                for k in range(8):
                    P.add("act", lambda e, k=k: e.activation(out=tmp_sq[:, k, 0:n], in_=xT[:, k, b0:b1], func=AF.Square),
                          reads=[f"xT{k}"], writes=[f"tsq{k}"])
                    P.add("pe", lambda e, k=k: e.matmul(pt[:, 0:n], lhsT=onesb, rhs=tmp_sq[:, k, 0:n], start=(k == 0), stop=(k == 7)),
                          reads=[f"tsq{k}", "onesb"], writes=[pt.name])

            def norm(bi):
                b0, b1 = OB[bi]
                n = b1 - b0
                pt = pts[bi]
                rs = tmp_rs[bi % 2]
                rk = f"trs{bi % 2}"
                rstd_ps(pt[:, 0:n], 1.0 / D, rs[:, 0:n], [pt.name], rk, "act")
                for k in range(8):
                    P.add("dve", lambda e, k=k: e.scalar_tensor_tensor(out=hdst(k, b0, b1), in0=xT[:, k, b0:b1],
                                                                       scalar=vecs[:, gcol0 + k:gcol0 + k + 1], in1=rs[:, 0:n],
                                                                       op0=ALU.mult, op1=ALU.mult),
                          reads=[f"xT{k}", rk, "vecs"], writes=[f"{hkey}{bi}"])
            stats(0)
            for bi in range(5):
                if bi + 1 < 5:
                    stats(bi + 1)
                norm(bi)

        FB = [(0, 510), (510, 1020), (1020, 1530), (1530, 2040), (2040, NOWN)]
        GSZ = 4

        def ffn(layer):
            P.barrier(lambda e: e.memset(onesf[:, 0:8], 1.0))
            A.reset()
            h2 = A.alloc(BF16, 8, NOWN + 2)
            tsq = A.alloc(BF16, 8, 512)
            trs = [A.alloc(F32, 512) for _ in range(2)]
            actT = A.alloc(BF16, GSZ, NOWN)
            wu = [A.alloc(BF16, 8, 2 * GSZ * 128) for _ in range(2)]
            wd = [A.alloc(BF16, GSZ, D) for _ in range(2)]
            ta = [A.alloc(F32, 512) for _ in range(2)]
            tg = [A.alloc(F32, 512) for _ in range(2)]
            tsl = [A.alloc(F32, 512) for _ in range(2)]
            P.add("pool", lambda e: e.memset(h2[:, :, 0:2], 0.0), writes=["h2pad"])
            rms_feat(V_FFNG + 8 * layer, lambda k, b0, b1: h2[:, k, 2 + b0:2 + b1], "h2b", tsq, trs)
            P.add("pool", lambda e: e.tensor_tensor(out=h2[:, :, 2:2 + HALO], in0=h2[:, :, 2:2 + HALO],
                                                    in1=vmaskb[:, 0:HALO].unsqueeze(1).to_broadcast([128, 8, HALO]), op=ALU.mult),
                  reads=["h2b0", "vmaskb"], writes=["h2b0"])
            def h2k(f0, f1):
                ks = ["h2pad"]
                for bi, (b0, b1) in enumerate(OB):
                    if b0 < f1 and b1 > f0 - 2:
                        ks.append(f"h2b{bi}")
                return ks
            ngrp = (22 + GSZ - 1) // GSZ
            cstate = {"cnt": 0}

            def gload(g):
                i = g % 2
                j0 = g * GSZ
                gs = min(GSZ, 22 - j0)
                for jj in range(gs):
                    j = j0 + jj
                    wload(wu[i][:, :, jj * 128:(jj + 1) * 128], wsrc(w_up[layer], j * 128, (j + 1) * 128), f"wu{i}a{jj}")
                    wload(wu[i][:, :, (GSZ + jj) * 128:(GSZ + jj + 1) * 128], wsrc(w_up[layer], FFN + j * 128, FFN + (j + 1) * 128), f"wu{i}g{jj}")
                wload(wd[i][:, 0:gs, :], w_dn[layer][j0 * 128:(j0 + gs) * 128, :].rearrange("(j p) d -> p j d", p=128), f"wd{i}")

            def make_unit(g, jj, f0, f1):
                i = g % 2
                j = g * GSZ + jj
                n = f1 - f0
                st_ = {}

                def front():
                    pa = nextps(PSALL)
                    pg = nextps(PSALL)
                    proj(pa, n + 2, lambda k: wu[i][:, k, jj * 128:(jj + 1) * 128],
                         lambda k: h2[:, k, f0:f1 + 2], h2k(f0, f1) + [f"wu{i}a{jj}"])
                    proj(pg, n + 2, lambda k: wu[i][:, k, (GSZ + jj) * 128:(GSZ + jj + 1) * 128],
                         lambda k: h2[:, k, f0:f1 + 2], h2k(f0, f1) + [f"wu{i}g{jj}"])
                    c = cstate["cnt"] % 2
                    cstate["cnt"] += 1
                    st_["c"] = c
                    for (pp, tt, col, nm) in ((pa, ta[c], j, "ta"), (pg, tg[c], 22 + j, "tg")):
                        fc = fcv[:, layer * 44 + col, :]
                        P.add("act", lambda e, pp=pp, tt=tt, fc=fc: e.activation(out=tt[:, 0:n], in_=pp[:, 2:n + 2], func=AF.Copy, scale=fc[:, 2:3]),
                              reads=[pp.name, "fcv"], writes=[f"{nm}{c}"])
                        P.add("dve", lambda e, pp=pp, tt=tt, fc=fc: e.scalar_tensor_tensor(out=tt[:, 0:n], in0=pp[:, 1:n + 1], scalar=fc[:, 1:2], in1=tt[:, 0:n],
                                                                                         op0=ALU.mult, op1=ALU.add),
                              reads=[pp.name, f"{nm}{c}", "fcv"], writes=[f"{nm}{c}"])
                        P.add("dve", lambda e, pp=pp, tt=tt, fc=fc: e.scalar_tensor_tensor(out=tt[:, 0:n], in0=pp[:, 0:n], scalar=fc[:, 0:1], in1=tt[:, 0:n],
                                                                                         op0=ALU.mult, op1=ALU.add),
                              reads=[pp.name, f"{nm}{c}", "fcv"], writes=[f"{nm}{c}"])

                def tail():
                    c = st_["c"]
                    P.add("act", lambda e: e.activation(out=tsl[c][:, 0:n], in_=tg[c][:, 0:n], func=AF.Silu),
                          reads=[f"tg{c}"], writes=[f"tsl{c}"])
                    P.add("pool", lambda e: e.tensor_tensor(out=actT[:, jj, f0:f1], in0=ta[c][:, 0:n], in1=tsl[c][:, 0:n], op=ALU.mult),
                          reads=[f"ta{c}", f"tsl{c}"], writes=[f"actT{jj}"])
                return front, tail

            def units_of(g):
                gs = min(GSZ, 22 - g * GSZ)
                return [make_unit(g, jj, f0, f1) for jj in range(gs) for (f0, f1) in FB]

            def down(g):
                i = g % 2
                gs = min(GSZ, 22 - g * GSZ)
                for dc in range(8):
                    for (b0, b1) in OB:
                        n = b1 - b0
                        pt = nextps(PSALL)

                        def dmm(e, pt=pt, n=n, dc=dc, b0=b0, b1=b1):
                            for jj in range(gs):
                                r = e.matmul(pt[:, 0:n], lhsT=wd[i][:, jj, dc * 128:(dc + 1) * 128], rhs=actT[:, jj, b0:b1], start=(jj == 0), stop=(jj == gs - 1))
                            return r
                        P.add("pe", dmm, reads=[f"actT{jj}" for jj in range(gs)] + [f"wd{i}"], writes=[pt.name])
                        P.add("dve", lambda e, pt=pt, dc=dc, b0=b0, b1=b1, n=n: e.tensor_tensor(out=xT[:, dc, b0:b1], in0=xT[:, dc, b0:b1], in1=pt[:, 0:n], op=ALU.add),
                              reads=[pt.name, f"xT{dc}"], writes=[f"xT{dc}"])

            LA = 2
            gload(0)
            cur = units_of(0)
            skip = 0
            pend = []
            for g in range(ngrp):
                if g + 1 < ngrp:
                    gload(g + 1)
                for (front, tail) in cur[skip:]:
                    front()
                    if pend:
                        pend.pop()()
                    pend.append(tail)
                nxt = units_of(g + 1) if g + 1 < ngrp else None
                held = []
                if nxt:
                    f1_, t1_ = nxt[0]
                    f1_()
                    pend.pop()()
                    f2_, t2_ = nxt[1]
                    f2_()
                    held = [t1_, t2_]
                else:
                    pend.pop()()
                down(g)
                if nxt:
                    held[0]()
                    pend.append(held[1])
                    cur = nxt
                    skip = LA

        ffn(0)
        dump("x1", xT[:, 0, :], ["xT0"])

        P.barrier(lambda e: e.memset(onesf[:, 0:8], 1.0))
        A.reset()
        h3 = A.alloc(BF16, 8, NOWN)
        at67 = A.alloc(BF16, 2, NOWN)
        vT = A.alloc(BF16, 6, NOWN)
        markL1 = A.top
        tsq1 = A.alloc(BF16, 8, 512)
        trs1 = [A.alloc(F32, 512) for _ in range(2)]
        rcp1 = A.alloc(F32, 512)
        pbufs1 = [[A.alloc(BF16, 512), A.alloc(BF16, 512)] for _ in range(3)]
        sqb1 = A.alloc(BF16, 512)
        rsb1 = A.alloc(F32, 512)
        qTp1 = [A.alloc(BF16, NOWN) for _ in range(2)]
        wp1 = [A.alloc(BF16, 8, 128) for _ in range(2)]
        rms_feat(V_MIXG + 8, lambda k, b0, b1: h3[:, k, b0:b1], "h3b", tsq1, trs1)
        h3keys = [f"h3b{bi}" for bi in range(5)]
        zb1 = A.alloc(F32, 1)
        P.add("dve", lambda e: e.memset(zb1, 0.0), writes=["zb"])

        def attnT1(k, b0, b1):
            return h3[:, k, b0:b1] if k < 6 else at67[:, k - 6, b0:b1]

        h3src = lambda k, b0, b1: h3[:, k, b0:b1]
        h3kf = lambda b0, b1: h3keys
        for c in range(2):
            nl, nch = cross_work(1, c, w_cv, 1536, h3src, h3kf, wp1, qTp1, sqb1, rsb1, banks=list(ps))
            nl()
            for f_ in nch:
                f_()
            cross_att(1, c, qTp1, at67, pbufs1, rcp1, zb1, "b1", obanks=[(ps[4], ps[5]), (ps[6], ps[7])])
        P.barrier(lambda e: e.memset(onesf[:, 0:8], 1.0))
        A.top = markL1
        up_ = [A.alloc(BF16, NOWN + 30) for _ in range(2)]
        dg = [A.alloc(BF16, 31, 128) for _ in range(2)]
        wcv = [A.alloc(BF16, 8, 256) for _ in range(2)]
        sg = [A.alloc(F32, 512) for _ in range(2)]
        cacc = [A.alloc(F32, 512) for _ in range(2)]
        ccnt = [0]
        NPE = 27
        for i in range(2):
            P.add("pool", lambda e, i=i: e.memset(up_[i][:, 0:30], 0.0), writes=[f"up{i}pad"])
        for c in range(6):
            i = c % 2
            wload(wcv[i][:, :, 0:128], wsrc(w_cv, c * 128, (c + 1) * 128), f"wcv{i}a")
            wload(wcv[i][:, :, 128:256], wsrc(w_cv, 768 + c * 128, 768 + (c + 1) * 128), f"wcv{i}g")
            for k in range(NPE):
                P.add("pool", lambda e, i=i, c=c, k=k: e.tensor_scalar(out=dg[i][:, k, :], in0=identb, scalar1=cw[:, c, k:k + 1], scalar2=0.0, op0=ALU.mult, op1=ALU.add),
                      reads=["identb", "cw"], writes=[f"dg{i}"])
            for bi, (b0, b1) in enumerate(OB):
                n = b1 - b0
                pa = nextps(PSALL)
                pg = nextps(PSALL)
                proj(pa, n, lambda k, i=i: wcv[i][:, k, 0:128], lambda k, b0=b0, b1=b1: h3[:, k, b0:b1], [h3keys[bi], f"wcv{i}a"])
                proj(pg, n, lambda k, i=i: wcv[i][:, k, 128:256], lambda k, b0=b0, b1=b1: h3[:, k, b0:b1], [h3keys[bi], f"wcv{i}g"])
                s_ = sg[bi % 2]
                P.add("act", lambda e, pg=pg, s_=s_, n=n: e.activation(out=s_[:, 0:n], in_=pg[:, 0:n], func=AF.Sigmoid),
                      reads=[pg.name], writes=[f"sg{bi % 2}"])
                P.add("dve", lambda e, pa=pa, s_=s_, n=n, i=i, b0=b0, b1=b1: e.tensor_tensor(out=up_[i][:, 30 + b0:30 + b1], in0=pa[:, 0:n], in1=s_[:, 0:n], op=ALU.mult),
                      reads=[pa.name, f"sg{bi % 2}"], writes=[f"up{i}b{bi}"])
                if bi == 0:
                    P.add("pool", lambda e, i=i: e.tensor_tensor(out=up_[i][:, 30:30 + HALO], in0=up_[i][:, 30:30 + HALO], in1=vmaskb[:, 0:HALO], op=ALU.mult),
                          reads=[f"up{i}b0", "vmaskb"], writes=[f"up{i}b0"])
            for bi, (b0, b1) in enumerate(OB):
                n = b1 - b0
                pt = nextps(PSALL)

                def cmm(e, pt=pt, n=n, i=i, b0=b0, b1=b1):
                    for k in range(NPE):
                        r = e.matmul(pt[:, 0:n], lhsT=dg[i][:, k, :], rhs=up_[i][:, b0 + k:b1 + k], start=(k == 0), stop=(k == NPE - 1))
                    return r
                ukeys = [f"up{i}b{x}" for x in range(max(0, bi - 1), bi + 1)] + [f"up{i}pad"]
                P.add("pe", cmm, reads=ukeys + [f"dg{i}"], writes=[pt.name])
                ac = cacc[ccnt[0] % 2]
                ak = f"cacc{ccnt[0] % 2}"
                ccnt[0] += 1
                for k in range(NPE, 31):
                    if k == NPE:
                        P.add("dve", lambda e, ac=ac, n=n, i=i, b0=b0, b1=b1, c=c, k=k: e.tensor_scalar(
                            out=ac[:, 0:n], in0=up_[i][:, b0 + k:b1 + k], scalar1=cw[:, c, k:k + 1], scalar2=vecs[:, V_CVB + c:V_CVB + c + 1],
                            op0=ALU.mult, op1=ALU.add), reads=ukeys + ["cw", "vecs"], writes=[ak])
                    else:
                        P.add("dve", lambda e, ac=ac, n=n, i=i, b0=b0, b1=b1, c=c, k=k: e.scalar_tensor_tensor(
                            out=ac[:, 0:n], in0=up_[i][:, b0 + k:b1 + k], scalar=cw[:, c, k:k + 1], in1=ac[:, 0:n],
                            op0=ALU.mult, op1=ALU.add), reads=ukeys + ["cw", ak], writes=[ak])
                P.add("dve", lambda e, pt=pt, ac=ac, n=n, c=c, b0=b0, b1=b1: e.tensor_tensor(out=vT[:, c, b0:b1], in0=pt[:, 0:n], in1=ac[:, 0:n], op=ALU.add),
                      reads=[pt.name, ak], writes=[f"vT{c}b{bi}"])
        P.barrier(lambda e: e.memset(onesf[:, 0:8], 1.0))
        A.top = markL1
        vsq = A.alloc(BF16, 6, 512)
        lmean = A.alloc(F32, 512)
        lm2 = A.alloc(F32, 512)
        lrs = A.alloc(F32, 512)
        ltmp = [A.alloc(F32, 512) for _ in range(2)]
        wo1 = A.alloc(BF16, 8, D)
        for bi, (b0, b1) in enumerate(OB):
            n = b1 - b0
            p1 = nextps(PSALL)
            p2 = nextps(PSALL)
            for c in range(6):
                P.add("act", lambda e, c=c, b0=b0, b1=b1, n=n: e.activation(out=vsq[:, c, 0:n], in_=vT[:, c, b0:b1], func=AF.Square),
                      reads=[f"vT{c}b{bi}"], writes=[f"vsq{c}"])
                P.add("pe", lambda e, c=c, b0=b0, b1=b1, n=n, p1=p1: e.matmul(p1[:, 0:n], lhsT=onesb, rhs=vT[:, c, b0:b1], start=(c == 0), stop=(c == 5)),
                      reads=[f"vT{c}b{bi}", "onesb"], writes=[p1.name])
                P.add("pe", lambda e, c=c, n=n, p2=p2: e.matmul(p2[:, 0:n], lhsT=onesb, rhs=vsq[:, c, 0:n], start=(c == 0), stop=(c == 5)),
                      reads=[f"vsq{c}", "onesb"], writes=[p2.name])
            P.add("act", lambda e, p1=p1, n=n: e.activation(out=lmean[:, 0:n], in_=p1[:, 0:n], func=AF.Copy, scale=1.0 / 768), reads=[p1.name], writes=["lmean"])
            P.add("dve", lambda e, n=n: e.tensor_tensor(out=lm2[:, 0:n], in0=lmean[:, 0:n], in1=lmean[:, 0:n], op=ALU.mult), reads=["lmean"], writes=["lm2"])
            P.add("dve", lambda e, p2=p2, n=n: e.scalar_tensor_tensor(out=lm2[:, 0:n], in0=p2[:, 0:n], scalar=1.0 / 768, in1=lm2[:, 0:n], op0=ALU.mult, op1=ALU.subtract),
                  reads=[p2.name, "lm2"], writes=["lm2"])
            P.add("act", lambda e, n=n: e.activation(out=lrs[:, 0:n], in_=lm2[:, 0:n], func=AF.Sqrt, bias=epsb), reads=["lm2", "epsb"], writes=["lrs"])
            P.add("dve", lambda e, n=n: e.reciprocal(out=lrs[:, 0:n], in_=lrs[:, 0:n]), reads=["lrs"], writes=["lrs"])
            for c in range(6):
                lt = ltmp[c % 2]
                P.add("dve", lambda e, c=c, lt=lt, b0=b0, b1=b1, n=n: e.tensor_tensor(out=lt[:, 0:n], in0=vT[:, c, b0:b1], in1=lmean[:, 0:n], op=ALU.subtract),
                      reads=[f"vT{c}b{bi}", "lmean"], writes=[f"ltmp{c % 2}"])
                P.add("pool", lambda e, lt=lt, n=n: e.tensor_tensor(out=lt[:, 0:n], in0=lt[:, 0:n], in1=lrs[:, 0:n], op=ALU.mult),
                      reads=[f"ltmp{c % 2}", "lrs"], writes=[f"ltmp{c % 2}"])
                P.add("act", lambda e, c=c, lt=lt, b0=b0, b1=b1, n=n: e.activation(out=h3[:, c, b0:b1], in_=lt[:, 0:n], func=AF.Silu,
                                                                                  scale=vecs[:, V_LNG + c:V_LNG + c + 1], bias=vecs[:, V_LNB + c:V_LNB + c + 1]),
                      reads=[f"ltmp{c % 2}", "vecs"] + h3keys, writes=[f"tok{c}b{bi}"])
            if bi == 0:
                for half in range(2):
                    wload(wo1[:, :, half * 512:(half + 1) * 512], wsrc(w_out[1], half * 512, (half + 1) * 512), f"wo{half}")
            if bi >= 1:
                outproj_block(wo1, attnT1, [f"tok{c}b{bi - 1}" for c in range(6)] + ["attnT6", "attnT7"], bi - 1)
        outproj_block(wo1, attnT1, [f"tok{c}b4" for c in range(6)] + ["attnT6", "attnT7"], 4)

        dump("x1p", xT[:, 0, :], ["xT0"])

        ffn(1)

        P.barrier(lambda e: e.memset(onesf[:, 0:8], 1.0))
        A.reset()
        ob = [A.alloc(F32, D) for _ in range(3)]
        for t in range(16):
            o_ = ob[t % 3]
            for k2 in range(2):
                pt = nextps(PSALL)

                def tr3(e, pt=pt, t=t, k2=k2):
                    for j in range(4):
                        k = 4 * k2 + j
                        r = e.transpose(pt[:, j * 128:(j + 1) * 128], xT[:, k, HALO + t * 128:HALO + (t + 1) * 128], identf)
                    return r
                P.add("pe", tr3, reads=[f"xT{4 * k2 + j}" for j in range(4)] + ["identf"], writes=[pt.name])
                if k2 == 0:
                    P.add("act", lambda e, pt=pt, o_=o_: e.copy(out=o_[:, 0:512], in_=pt[:, :]), reads=[pt.name], writes=[f"ob{t % 3}a"])
                else:
                    P.add("dve", lambda e, pt=pt, o_=o_: e.tensor_copy(out=o_[:, 512:1024], in_=pt[:, :]), reads=[pt.name], writes=[f"ob{t % 3}b"])
            P.add("sp", lambda e, t=t, o_=o_: e.dma_start(out=out_d[t * 128:(t + 1) * 128, :], in_=o_), reads=[f"ob{t % 3}a", f"ob{t % 3}b"], dma=True, out=True)

        P.emit(nc, st)
    return nc


def _cols(v):
    v = np.asarray(v, np.float32)
    return np.ascontiguousarray(v.reshape(-1, 128).T)


def _rep64(v):
    v = np.asarray(v, np.float32)
    return np.concatenate([v, v])[:, None]


def make_in_maps(inp):
    x = np.asarray(inp["x"], np.float32)
    mem = np.asarray(inp["mem"], np.float32)
    vecs = np.zeros((128, 160), np.float32)
    vecs[:, 0:8] = _cols(inp["mix_norm_g"][0])
    vecs[:, 8:16] = _cols(inp["mix_norm_g"][1])
    vecs[:, 16:24] = _cols(inp["ffn_norm_g"][0])
    vecs[:, 24:32] = _cols(inp["ffn_norm_g"][1])
    vecs[:, 32:40] = _cols(inp["mem_norm_g"])
    vecs[:, 40:41] = _rep64(inp["cross_q_g"][0])
    vecs[:, 41:42] = _rep64(inp["cross_q_g"][1])
    vecs[:, 42:43] = _rep64(inp["cross_k_g"][0])
    vecs[:, 43:44] = _rep64(inp["cross_k_g"][1])
    vecs[:, 44:45] = _rep64(inp["fox_q_g"][0])
    vecs[:, 45:46] = _rep64(inp["fox_k_g"][0])
    vecs[:, 46:52] = _cols(inp["conv_dw_b"][0])
    vecs[:, 52:58] = _cols(inp["conv_ln_g"][0])
    vecs[:, 58:64] = _cols(inp["conv_ln_b"][0])
    vecs[0:12, 64] = np.asarray(inp["fox_b_f"], np.float32)[0]
    cwv = np.asarray(inp["conv_dw"], np.float32)[0]
    cw = np.ascontiguousarray(cwv.reshape(31, 6, 128).transpose(2, 1, 0)).reshape(128, 6 * 31)
    fc = np.asarray(inp["ffn_conv"], np.float32)
    fcv = np.ascontiguousarray(fc.reshape(2, 3, 44, 128).transpose(3, 0, 2, 1)).reshape(128, 2 * 44 * 3)
    identf = np.eye(128, dtype=np.float32)
    cmask = np.triu(np.ones((128, 128), np.float32))
    shared = {
        "vecs": vecs, "cw": cw, "fcv": fcv, "identf": identf, "cmask": cmask,
        "mem_w_kv": np.asarray(inp["mem_w_kv"], np.float32),
        "mix_w_out": np.asarray(inp["mix_w_out"], np.float32),
        "fox_w_in": np.ascontiguousarray(np.asarray(inp["fox_w_in"], np.float32)[0]),
        "conv_w_in": np.ascontiguousarray(np.asarray(inp["conv_w_in"], np.float32)[0]),
        "ffn_w_up": np.asarray(inp["ffn_w_up"], np.float32),
        "ffn_w_down": np.asarray(inp["ffn_w_down"], np.float32),
    }
    maps = []
    for core in range(8):
        b, h = divmod(core, 2)
        if h == 0:
            xc = np.concatenate([np.zeros((2048, D), np.float32), x[b, 0:2048]], axis=0)
            km = np.zeros((128, 32), np.float32)
            km[:, 0:16] = NEG
            vm = np.zeros((128, 128), np.float32)
        else:
            xc = x[b]
            km = np.zeros((128, 32), np.float32)
            vm = np.ones((128, 128), np.float32)
        m = dict(shared)
        m["xctx"] = np.ascontiguousarray(xc)
        m["mem"] = np.ascontiguousarray(mem[b])
        m["kmask"] = km
        m["vmask"] = vm
        maps.append(m)
    return maps


_NC_CACHE = {}


def kernel(**inputs):
    if "nc" not in _NC_CACHE:
        _NC_CACHE["nc"] = build_nc()
    nc = _NC_CACHE["nc"]
    maps = make_in_maps(inputs)
    res = run_bass_kernel_spmd(nc, maps, core_ids=list(range(8)))
    out = np.zeros((4, 4096, D), np.float32)
    for core in range(8):
        b, h = divmod(core, 2)
        out[b, h * 2048:(h + 1) * 2048] = res.results[core]["out"]
    return out
```

```python
from contextlib import ExitStack
import numpy as np
import concourse.bass as bass
import concourse.mybir as mybir
from concourse.bass_utils import run_bass_kernel_spmd

F32 = mybir.dt.float32
BF16 = mybir.dt.bfloat16
AF = mybir.ActivationFunctionType
ALU = mybir.AluOpType

ENGS = ("pe", "act", "dve", "pool", "sp")
EPOCH = 12000
N_DMA_SEMS = 24

D = 1024
NCTX = 4096
NOWN = 2112
HALO = 64
OWN0 = NCTX - NOWN
EPS = 1e-6
FFN = 2816
NEG = -30000.0
OB = [(0, 64), (64, 576), (576, 1088), (1088, 1600), (1600, 2112)]


class Op:
    __slots__ = ("eng", "fn", "deps", "sig", "dma", "ticket", "idx", "prev_dma")

    def __init__(self, eng, fn, dma):
        self.eng = eng
        self.fn = fn
        self.dma = dma
        self.deps = []
        self.sig = False
        self.ticket = None
        self.prev_dma = None


class Prog:
    def __init__(self):
        self.ops = {e: [] for e in ENGS}
        self.lastw = {}
        self.readers = {}
        self.n = 0
        self.out_dmas = []

    def add(self, eng, fn, reads=(), writes=(), dma=False, out=False):
        op = Op(eng, fn, dma)
        op.idx = self.n
        self.n += 1
        reads = list(reads) + ["__phase"]
        deps = {}
        for k in reads:
            w = self.lastw.get(k)
            if w is not None:
                deps[w.idx] = (w, True)
        for k in writes:
            w = self.lastw.get(k)
            if w is not None and w.idx not in deps:
                deps[w.idx] = (w, False)
            for r in self.readers.get(k, ()):
                if r.idx not in deps:
                    deps[r.idx] = (r, False)
        for d, raw in deps.values():
            if d is op:
                continue
            if d.eng == op.eng and not d.dma and not op.dma and not raw:
                continue
            if d.eng == "pe" and op.eng == "pe" and not d.dma and not op.dma:
                continue
            op.deps.append(d)
            d.sig = True
        for k in reads:
            self.readers.setdefault(k, []).append(op)
        for k in writes:
            self.lastw[k] = op
            self.readers[k] = []
        self.ops[eng].append(op)
        if out:
            op.sig = True
            self.out_dmas.append(op)
        return op

    def barrier(self, fn):
        op = Op("dve", fn, False)
        op.idx = self.n
        self.n += 1
        seen = set()
        for k, rs in self.readers.items():
            for r in rs:
                if r.idx not in seen:
                    seen.add(r.idx)
                    op.deps.append(r)
                    r.sig = True
        for k, w in self.lastw.items():
            if w.idx not in seen:
                seen.add(w.idx)
                op.deps.append(w)
                w.sig = True
        self.lastw = {"__phase": op}
        self.readers = {}
        self.ops["dve"].append(op)
        return op

    def emit(self, nc, stack):
        sems = {}
        for e in ENGS:
            cnt = 0
            for op in self.ops[e]:
                if op.dma or not op.sig:
                    continue
                ep, c = divmod(cnt, EPOCH)
                op.ticket = ((e, ep), c + 1)
                cnt += 1
            for ep in range((cnt + EPOCH - 1) // EPOCH):
                sems[(e, ep)] = stack.enter_context(nc.semaphore(f"s_{e}{ep}"))
        for j in range(N_DMA_SEMS):
            sems[("dma", j)] = stack.enter_context(nc.semaphore(f"s_dma{j}"))
        all_dma = sorted([op for e in ENGS for op in self.ops[e] if op.dma], key=lambda o: o.idx)
        last_on = [None] * N_DMA_SEMS
        cnt_on = [0] * N_DMA_SEMS
        for i, op in enumerate(all_dma):
            s = i % N_DMA_SEMS
            op.prev_dma = last_on[s]
            cnt_on[s] += 16
            op.ticket = (("dma", s), cnt_on[s])
            last_on[s] = op
        block = stack.enter_context(nc.Block())
        prog = self

        def run(e):
            def body(eng):
                waited = {}

                def wait(t):
                    key, val = t
                    if key[0] != "dma":
                        cur = waited.get(key[0])
                        if cur is not None and (cur[0] > key[1] or (cur[0] == key[1] and cur[1] >= val)):
                            return
                        waited[key[0]] = (key[1], val)
                    else:
                        if waited.get(key, 0) >= val:
                            return
                        waited[key] = val
                    eng.wait_ge(sems[key], val)

                for op in prog.ops[e]:
                    best = {}
                    for d in op.deps:
                        k0 = d.ticket[0]
                        kk = k0 if k0[0] == "dma" else k0[0]
                        cur = best.get(kk)
                        if cur is None or (d.ticket[0][1], d.ticket[1]) > (cur[0][1], cur[1]) or k0[0] == "dma" and d.ticket[1] > cur[1]:
                            best[kk] = d.ticket
                    for t in best.values():
                        wait(t)
                    if op.dma and op.prev_dma is not None:
                        wait(op.prev_dma.ticket)
                    ins = op.fn(eng)
                    if op.dma:
                        ins.then_inc(sems[op.ticket[0]], 16)
                    elif op.sig:
                        ins.then_inc(sems[op.ticket[0]], 1)
                if e == "sp":
                    for op in prog.out_dmas:
                        wait(op.ticket)
            return body

        block.tensor(run("pe"))
        block.scalar(run("act"))
        block.vector(run("dve"))
        block.gpsimd(run("pool"))
        block.sync(run("sp"))


ARENA_WORDS = 52800


class Arena:
    def __init__(self, ap):
        self.ap = ap
        self.base = 0
        self.top = 0
        self.limit = ARENA_WORDS

    def alloc(self, dt, *dims):
        n = int(np.prod(dims))
        words = n if dt == F32 else (n + 1) // 2
        words = (words + 7) // 8 * 8
        off = self.top
        self.top += words
        assert self.top <= self.limit, f"arena overflow {self.top} > {self.limit}"
        v = self.ap[:, off:off + words]
        if dt != F32:
            v = v.bitcast(dt)
        v = v[:, 0:n]
        if len(dims) == 2:
            v = v.rearrange("p (a b) -> p a b", a=dims[0], b=dims[1])
        elif len(dims) == 3:
            v = v.rearrange("p (a b c) -> p a b c", a=dims[0], b=dims[1], c=dims[2])
        return v

    def mark_persistent(self):
        self.base = self.top

    def reset(self):
        self.top = self.base


def build_nc(dbg=None):
    dbg = dbg or {}
    nc = bass.Bass("TRN2", target_bir_lowering=False)

    def din(name, shape):
        return nc.dram_tensor(name, list(shape), F32, kind="ExternalInput").ap()

    xctx = din("xctx", [NCTX, D])
    mem = din("mem", [256, D])
    kmask_d = din("kmask", [128, 32])
    vmask_d = din("vmask", [128, 128])
    vecs_d = din("vecs", [128, 160])
    cw_d = din("cw", [128, 6 * 31])
    fcv_d = din("fcv", [128, 2 * 44 * 3])
    identf_d = din("identf", [128, 128])
    mask_d = din("cmask", [128, 128])
    w_kv = din("mem_w_kv", [D, 512])
    w_out = din("mix_w_out", [2, D, D])
    w_fox = din("fox_w_in", [D, 2572])
    w_cv = din("conv_w_in", [D, 1792])
    w_up = din("ffn_w_up", [2, D, 2 * FFN])
    w_dn = din("ffn_w_down", [2, FFN, D])
    out_d = nc.dram_tensor("out", [2048, D], F32, kind="ExternalOutput").ap()
    dbg_out = {}
    for name, shape in dbg.items():
        dbg_out[name] = nc.dram_tensor("dbg_" + name, list(shape), F32, kind="ExternalOutput").ap()

    st = ExitStack()
    with st:
        arena_t = st.enter_context(nc.sbuf_tensor("arena", [128, ARENA_WORDS], F32))
        A = Arena(arena_t[:])
        class PS:
            def __init__(self, t, name):
                self.t = t
                self.name = name

            def __getitem__(self, idx):
                return self.t[idx]
        ps = [PS(st.enter_context(nc.psum_tensor(f"ps{i}", [128, 512], F32)), f"ps{i}") for i in range(8)]
        P = Prog()
        rot = {"i": 0}

        def nextps(lst):
            rot["i"] += 1
            return lst[rot["i"] % len(lst)]

        identf = A.alloc(F32, 128)
        identb = A.alloc(BF16, 128)
        onesb = A.alloc(BF16, 128)
        bdiagb = A.alloc(BF16, 128)
        cmaskb = A.alloc(BF16, 128)
        onesf = A.alloc(F32, 512)
        vecs = A.alloc(F32, 160)
        cw = A.alloc(F32, 6, 31)
        fcv = A.alloc(F32, 2 * 44, 3)
        kmask = A.alloc(F32, 32)
        vmaskb = A.alloc(BF16, 128)
        negbf = A.alloc(F32, 1)
        V_MIXG, V_FFNG, V_MEMG = 0, 16, 32
        V_CQG, V_CKG, V_FQG, V_FKG = 40, 42, 44, 45
        V_CVB, V_LNG, V_LNB, V_FB = 46, 52, 58, 64

        P.add("sp", lambda e: e.dma_start(out=identf, in_=identf_d), writes=["identf"], dma=True)
        P.add("sp", lambda e: e.dma_start(out=vecs, in_=vecs_d), writes=["vecs"], dma=True)
        P.add("sp", lambda e: e.dma_start(out=cw.rearrange("p a b -> p (a b)"), in_=cw_d), writes=["cw"], dma=True)
        P.add("sp", lambda e: e.dma_start(out=fcv.rearrange("p a b -> p (a b)"), in_=fcv_d), writes=["fcv"], dma=True)
        P.add("sp", lambda e: e.dma_start(out=kmask, in_=kmask_d), writes=["kmask"], dma=True)
        P.add("pool", lambda e: e.dma_start(out=cmaskb, in_=mask_d), writes=["cmaskb"], dma=True)
        P.add("pool", lambda e: e.dma_start(out=vmaskb, in_=vmask_d), writes=["vmaskb"], dma=True)
        P.add("pool", lambda e: e.dma_start(out=identb, in_=identf_d), writes=["identb"], dma=True)
        P.add("dve", lambda e: e.memset(onesb, 1.0), writes=["onesb"])
        P.add("dve", lambda e: e.memset(onesf, 1.0), writes=["onesf"])
        P.add("dve", lambda e: e.memset(bdiagb, 0.0), writes=["bdiagb"])

        def bd2(e):
            e.memset(bdiagb[0:64, 0:64], 1.0)
            return e.memset(bdiagb[64:128, 64:128], 1.0)
        P.add("dve", bd2, writes=["bdiagb"])
        P.add("dve", lambda e: e.tensor_scalar(out=negbf[0:12, :], in0=vecs[0:12, V_FB:V_FB + 1], scalar1=-1.0,
                                               scalar2=0.0, op0=ALU.mult, op1=ALU.add), reads=["vecs"], writes=["negbf"])

        XT_OFF = ARENA_WORDS - 8 * NOWN
        xT = A.ap[:, XT_OFF:ARENA_WORDS].rearrange("p (a b) -> p a b", a=8, b=NOWN)
        ckT = A.alloc(BF16, 2, 2, 256)
        vmem = A.alloc(BF16, 2, 4, 128)
        A.mark_persistent()

        def dump(name, src_ap, reads):
            if name in dbg_out:
                P.add("pool", lambda e: e.dma_start(out=dbg_out[name], in_=src_ap), reads=reads, dma=True, out=True)

        def wload(dst, src, key, eng="pool"):
            P.add(eng, lambda e: e.dma_start(out=dst, in_=src), writes=[key], dma=True)

        def wsrc(w, c0, c1):
            return w.rearrange("(k p) n -> p k n", p=128)[:, :, c0:c1]

        def rstd_from(ps_t, rows, n, inv_n, out_ap, rkey, wkey):
            P.add("act", lambda e: e.activation(out=out_ap, in_=ps_t[rows, 0:n], func=AF.Sqrt, scale=inv_n, bias=epsb[rows, :]),
                  reads=[rkey, "epsb"], writes=[wkey])
            P.add("dve", lambda e: e.reciprocal(out=out_ap, in_=out_ap), reads=[wkey], writes=[wkey])

        epsb = A.alloc(F32, 1)
        nhalf = A.alloc(F32, 512)
        A.mark_persistent()
        P.add("dve", lambda e: e.memset(epsb, EPS), writes=["epsb"])
        P.add("pool", lambda e: e.memset(nhalf, -0.5), writes=["nhalf"])

        def rstd_ps(src, inv_n, out_ap, rkeys, wkey, evac):
            np_ = out_ap.shape[0]
            if evac == "dve":
                P.add("act", lambda e: e.activation(out=out_ap, in_=src, func=AF.Ln, scale=inv_n, bias=epsb[0:np_, :]),
                      reads=rkeys + ["epsb"], writes=[wkey])
                P.add("act", lambda e: e.activation(out=out_ap, in_=out_ap, func=AF.Exp, scale=-0.5), reads=[wkey], writes=[wkey])
            else:
                P.add("act", lambda e: e.activation(out=out_ap, in_=src, func=AF.Sqrt, scale=inv_n, bias=epsb[0:np_, :]),
                      reads=rkeys + ["epsb"], writes=[wkey])
                P.add("dve", lambda e: e.reciprocal(out=out_ap, in_=out_ap), reads=[wkey], writes=[wkey])

        PSG = [ps[6], ps[7]]
        memx = A.alloc(F32, 2, D)
        memT = A.alloc(F32, 8, 256)
        msq = A.alloc(BF16, 8, 256)
        mrs = A.alloc(F32, 256)
        hmT = A.alloc(BF16, 8, 256)
        wkv = A.alloc(BF16, 8, 512)
        kraw = A.alloc(F32, 2, 256)
        ksq = A.alloc(BF16, 256)
        krs = A.alloc(F32, 256)
        wload(wkv, wsrc(w_kv, 0, 512), "wkv")
        P.add("sp", lambda e: e.dma_start(out=memx, in_=mem.rearrange("(t p) d -> p t d", p=128)), writes=["memx"], dma=True)
        for t in range(2):
            for k in range(8):
                pt = nextps(ps)
                P.add("pe", lambda e, pt=pt, t=t, k=k: e.transpose(pt[:, 0:128], memx[:, t, k * 128:(k + 1) * 128], identf),
                      reads=["memx", "identf"], writes=[pt.name])
                P.add("act", lambda e, pt=pt, t=t, k=k: e.copy(out=memT[:, k, t * 128:(t + 1) * 128], in_=pt[:, 0:128]),
                      reads=[pt.name], writes=[f"memT{k}"])
        pt = nextps(ps)
        for k in range(8):
            P.add("act", lambda e, k=k: e.activation(out=msq[:, k, :], in_=memT[:, k, :], func=AF.Square),
                  reads=[f"memT{k}"], writes=[f"msq{k}"])
            P.add("pe", lambda e, k=k, pt=pt: e.matmul(pt[:, 0:256], lhsT=onesb, rhs=msq[:, k, :], start=(k == 0), stop=(k == 7)),
                  reads=[f"msq{k}", "onesb"], writes=[pt.name])
        rstd_from(pt, slice(0, 128), 256, 1.0 / D, mrs, pt.name, "mrs")
        for k in range(8):
            P.add("dve", lambda e, k=k: e.scalar_tensor_tensor(out=hmT[:, k, :], in0=memT[:, k, :], scalar=vecs[:, V_MEMG + k:V_MEMG + k + 1],
                                                              in1=mrs, op0=ALU.mult, op1=ALU.mult),
                  reads=[f"memT{k}", "mrs", "vecs"], writes=[f"hmT{k}"])
        for c in range(2):
            pt = nextps(ps)
            for k in range(8):
                P.add("pe", lambda e, k=k, c=c, pt=pt: e.matmul(pt[:, 0:256], lhsT=wkv[:, k, c * 128:(c + 1) * 128], rhs=hmT[:, k, :],
                                                               start=(k == 0), stop=(k == 7)),
                      reads=[f"hmT{k}", "wkv"], writes=[pt.name])
            P.add("act", lambda e, c=c, pt=pt: e.copy(out=kraw[:, c, :], in_=pt[:, 0:256]), reads=[pt.name], writes=[f"kraw{c}"])
            P.add("act", lambda e, c=c: e.activation(out=ksq, in_=kraw[:, c, :], func=AF.Square), reads=[f"kraw{c}"], writes=["ksq"])
            pt2 = nextps(ps)
            P.add("pe", lambda e, pt2=pt2: e.matmul(pt2[:, 0:256], lhsT=bdiagb, rhs=ksq, start=True, stop=True),
                  reads=["ksq", "bdiagb"], writes=[pt2.name])
            rstd_from(pt2, slice(0, 128), 256, 1.0 / 64, krs, pt2.name, "krs")
            for l in range(2):
                P.add("dve", lambda e, c=c, l=l: e.scalar_tensor_tensor(out=ckT[:, l, c, :], in0=kraw[:, c, :],
                                                                      scalar=vecs[:, V_CKG + l:V_CKG + l + 1], in1=krs,
                                                                      op0=ALU.mult, op1=ALU.mult),
                      reads=[f"kraw{c}", "krs", "vecs"], writes=[f"ckT{l}{c}"])
        P.add("dve", lambda e: e.memset(vmem.rearrange("p a b c -> p (a b c)"), 1.0), writes=["vmem"])
        for t in range(2):
            pt = nextps(ps)
            for k in range(8):
                P.add("pe", lambda e, k=k, t=t, pt=pt: e.matmul(pt[:, 0:256], lhsT=hmT[:, k, t * 128:(t + 1) * 128], rhs=wkv[:, k, 256:512],
                                                               start=(k == 0), stop=(k == 7)),
                      reads=[f"hmT{k}", "wkv"], writes=[pt.name])
            for h in range(4):
                lo = 0 if h % 2 == 0 else 64
                P.add("act", lambda e, t=t, h=h, lo=lo, pt=pt: e.copy(out=vmem[:, t, h, lo:lo + 64], in_=pt[:, h * 64:(h + 1) * 64]),
                      reads=[pt.name], writes=["vmem"])

        P.barrier(lambda e: e.memset(onesf[:, 0:8], 1.0))
        A.reset()
        hT = A.alloc(BF16, 8, NCTX)
        btab = A.alloc(F32, 12, 32, 9)
        mark1 = A.top
        xs = [A.alloc(F32, D) for _ in range(4)]
        junk = A.alloc(BF16, D)
        hn = [A.alloc(BF16, D) for _ in range(3)]
        ssq = A.alloc(F32, 32)
        rsd = A.alloc(F32, 32)
        wf = A.alloc(BF16, 8, 12)
        fe = A.alloc(F32, 512)
        cc = A.alloc(F32, NCTX)
        cT = A.alloc(F32, 32, 12)
        crefb = A.alloc(F32, 9, 12)
        tmpb = A.alloc(F32, 32)
        wload(wf, wsrc(w_fox, 2304, 2316), "wf")
        gmix0 = vecs[:, V_MIXG:V_MIXG + 8]
        p1pend = []
        for t in range(32):
            xb_ = xs[t % 4]
            hb_ = hn[t % 3]
            pt = nextps(ps[0:4])
            ptb = pt[:].bitcast(BF16).rearrange("p (a b) -> p a b", a=8, b=128)
            P.add("sp", lambda e, t=t, xb_=xb_: e.dma_start(out=xb_, in_=xctx[t * 128:(t + 1) * 128, :]), writes=[f"xs{t % 4}"], dma=True)
            P.add("act", lambda e, t=t, xb_=xb_: e.activation(out=junk, in_=xb_, func=AF.Square, accum_out=ssq[:, t:t + 1]),
                  reads=[f"xs{t % 4}"], writes=["junk", f"ssq{t}"])
            P.add("act", lambda e, t=t: e.activation(out=rsd[:, t:t + 1], in_=ssq[:, t:t + 1], func=AF.Sqrt, scale=1.0 / D, bias=epsb),
                  reads=[f"ssq{t}", "epsb"], writes=[f"rsd{t}"])
            P.add("dve", lambda e, t=t: e.reciprocal(out=rsd[:, t:t + 1], in_=rsd[:, t:t + 1]), reads=[f"rsd{t}"], writes=[f"rsd{t}"])
            P.add("dve", lambda e, t=t, xb_=xb_, hb_=hb_: e.tensor_scalar(out=hb_, in0=xb_, scalar1=rsd[:, t:t + 1], scalar2=0.0, op0=ALU.mult, op1=ALU.add),
                  reads=[f"xs{t % 4}", f"rsd{t}"], writes=[f"hn{t % 3}"])

            def tr(e, hb_=hb_, ptb=ptb):
                for k in range(8):
                    r = e.transpose(ptb[:, k, :], hb_[:, k * 128:(k + 1) * 128], identb)
                return r
            P.add("pe", tr, reads=[f"hn{t % 3}", "identb"], writes=[pt.name])
            if p1pend:
                p1pend.pop()()

            def evac(t=t, ptb=ptb, pt=pt):
                P.add("dve", lambda e: e.tensor_tensor(out=hT[:, :, t * 128:(t + 1) * 128], in0=ptb,
                                                       in1=gmix0.unsqueeze(2).to_broadcast([128, 8, 128]), op=ALU.mult),
                      reads=[pt.name, "vecs"], writes=[f"hT{t}"])
            p1pend.append(evac)
        p1pend.pop()()
        for cb in range(8):
            pt = nextps(PSG)

            def fmm(e, cb=cb, pt=pt):
                for k in range(8):
                    r = e.matmul(pt[0:12, :], lhsT=wf[:, k, :], rhs=hT[:, k, cb * 512:(cb + 1) * 512], start=(k == 0), stop=(k == 7))
                return r
            P.add("pe", fmm, reads=[f"hT{4 * cb + i}" for i in range(4)] + ["wf"], writes=[pt.name])
            P.add("act", lambda e, pt=pt: e.activation(out=fe[0:12, :], in_=pt[0:12, :], func=AF.Exp, scale=-1.0, bias=negbf[0:12, :]),
                  reads=[pt.name, "negbf"], writes=["fe"])
            P.add("act", lambda e: e.activation(out=fe[0:12, :], in_=fe[0:12, :], func=AF.Ln, bias=onesf[0:12, 0:1]), reads=["fe", "onesf"], writes=["fe"])
            init = 0.0 if cb == 0 else cc[0:12, cb * 512 - 1:cb * 512]
            P.add("dve", lambda e, cb=cb, init=init: e.tensor_tensor_scan(out=cc[0:12, cb * 512:(cb + 1) * 512], data0=onesf[0:12, :],
                                                                          data1=fe[0:12, :], initial=init, op0=ALU.mult, op1=ALU.subtract),
                  reads=["fe", "onesf", "cc"], writes=["cc"])
        pt = nextps(PSG)
        ptv = pt[:, 0:384].rearrange("p (a b) -> p a b", a=32, b=12)

        def ctr(e, ptv=ptv):
            for t in range(32):
                r = e.transpose(ptv[:, t, :], cc[0:12, t * 128:(t + 1) * 128], identf[0:12, 0:12])
            return r
        P.add("pe", ctr, reads=["cc", "identf"], writes=[pt.name])
        P.add("act", lambda e, ptv=ptv: e.copy(out=cT, in_=ptv), reads=[pt.name], writes=["cT"])
        pt2 = nextps(PSG)
        P.add("pe", lambda e, pt2=pt2: e.matmul(pt2[:, 0:108].rearrange("p (a b) -> p a b", a=9, b=12), lhsT=onesf[0:1, 0:128],
                                                rhs=cT[0:1, 15:32:2, :], start=True, stop=True),
              reads=["cT", "onesf"], writes=[pt2.name])
        P.add("act", lambda e, pt2=pt2: e.copy(out=crefb, in_=pt2[:, 0:108].rearrange("p (a b) -> p a b", a=9, b=12)),
              reads=[pt2.name], writes=["crefb"])
        for h in range(12):
            P.add("dve", lambda e, h=h: e.tensor_tensor(out=tmpb, in0=kmask, in1=cT[:, :, h], op=ALU.subtract),
                  reads=["kmask", "cT"], writes=["tmpb"])
            for sb in range(9):
                P.add("dve", lambda e, h=h, sb=sb: e.tensor_scalar(out=btab[:, h, :, sb], in0=tmpb, scalar1=crefb[:, sb, h:h + 1],
                                                                  scalar2=0.0, op0=ALU.add, op1=ALU.add),
                      reads=["tmpb", "crefb"], writes=["btab"])
        dump("hT", hT[:, 0, :], [f"hT{t}" for t in range(32)])
        dump("cc", cc[0:12, :], ["cc"])

        P.barrier(lambda e: e.memset(onesf[:, 0:8], 1.0))
        A.top = mark1
        attnT0 = A.alloc(BF16, 8, NOWN)
        mark_att = A.top
        SB_OF = [(0, HALO)] + [(HALO + 256 * i, HALO + 256 * (i + 1)) for i in range(8)]

        def qknorm_stages(pt_raw, n, gcol, out_ap, okey, sqb, rsb, tag, banks=None):
            p2 = nextps(banks or PSG)

            def s1():
                P.add("act", lambda e: e.activation(out=sqb[:, 0:n], in_=pt_raw[:, 0:n], func=AF.Square), reads=[pt_raw.name], writes=[tag + "sq"])

            def s2():
                P.add("pe", lambda e: e.matmul(p2[:, 0:n], lhsT=bdiagb, rhs=sqb[:, 0:n], start=True, stop=True),
                      reads=[tag + "sq", "bdiagb"], writes=[p2.name])

            def s3():
                rstd_ps(p2[:, 0:n], 1.0 / 64, rsb[:, 0:n], [p2.name], tag + "rs", "dve")

            def s4():
                P.add("dve", lambda e: e.scalar_tensor_tensor(out=out_ap, in0=pt_raw[:, 0:n], scalar=gcol, in1=rsb[:, 0:n], op0=ALU.mult, op1=ALU.mult),
                      reads=[pt_raw.name, tag + "rs", "vecs"], writes=[okey])
            return [s1, s2, s3, s4]

        def qknorm(pt_raw, n, gcol, out_ap, okey, sqb, rsb, tag):
            for f_ in qknorm_stages(pt_raw, n, gcol, out_ap, okey, sqb, rsb, tag):
                f_()

        def proj_half(pt, n, wsl, rhs_fn, rkeys, half):
            def f(e):
                for k in range(4 * half, 4 * half + 4):
                    r = e.matmul(pt[:, 0:n], lhsT=wsl(k), rhs=rhs_fn(k), start=(k == 0), stop=(k == 7))
                return r
            P.add("pe", f, reads=rkeys, writes=[pt.name])

        def proj(pt, n, wsl, rhs_fn, rkeys):
            def f(e):
                for k in range(8):
                    r = e.matmul(pt[:, 0:n], lhsT=wsl(k), rhs=rhs_fn(k), start=(k == 0), stop=(k == 7))
                return r
            P.add("pe", f, reads=rkeys, writes=[pt.name])

        def attention(qT, qkey, n_kt, kT_fn, kkey, v_fn, vkey, bias_fn, causal, dst_fn, dkey, pbufs, rcp, tag, fillers=(), every=5, obanks=None):
            SP = [[ps[0], ps[1]], [ps[2], ps[3]]]
            obanks = obanks or [(ps[4], ps[5])]

            def qk_f(e, kt, c0, n, b0, b1, sA, sB):
                kk = kT_fn(kt)
                e.matmul(sA[:, c0:n], lhsT=kk[0:64, :], rhs=qT[0:64, b0 + c0:b1], start=True, stop=True)
                return e.matmul(sB[:, c0:n], lhsT=kk[64:128, :], rhs=qT[64:128, b0 + c0:b1], start=True, stop=True)

            def ex_f(e, s_, p_, lo, hi, bias):
                return e.activation(out=p_[:, lo:hi], in_=s_[:, lo:hi], func=AF.Exp, scale=0.125, bias=bias)

            def mk_f(e, p_, m0, w):
                return e.tensor_tensor(out=p_[:, m0:m0 + w], in0=p_[:, m0:m0 + w], in1=cmaskb[:, 128 - w:128], op=ALU.mult)

            def pv_f(e, kt, c0, n, pA, pB, first, last, OA, OBk):
                vA, vB = v_fn(kt)
                e.matmul(OA[:, c0:n], lhsT=vA, rhs=pA[:, c0:n], start=first, stop=last)
                return e.matmul(OBk[:, c0:n], lhsT=vB, rhs=pB[:, c0:n], start=first, stop=last)

            def bind(f, *a):
                return lambda e: f(e, *a)

            fillers = list(fillers)
            nfill = len(fillers)
            popped = [0]
            step = [0]
            tot_steps = 0
            for bi_, (b0_, b1_) in enumerate(OB):
                qt0_ = 15 if bi_ == 0 else 16 + 4 * (bi_ - 1)
                tot_steps += (qt0_ + max(1, (b1_ - b0_) // 128)) if causal else n_kt
            tot_steps = max(1, tot_steps - 4)

            for bi, (b0, b1) in enumerate(OB):
                n = b1 - b0
                OA, OBk = obanks[bi % len(obanks)]
                qt0 = 15 if bi == 0 else 16 + 4 * (bi - 1)
                nt = max(1, n // 128)
                last_kt = (qt0 + nt - 1) if causal else n_kt - 1
                kts = list(range(last_kt + 1))
                sbs = [0] if bi == 0 else [2 * bi - 1, 2 * bi]

                def cols_for(kt, qt0=qt0):
                    if not causal:
                        return 0
                    return max(0, kt - qt0) * 128

                def qk(kt, i):
                    sA, sB = SP[i % 2]
                    P.add("pe", bind(qk_f, kt, cols_for(kt), n, b0, b1, sA, sB), reads=[qkey, kkey], writes=[sA.name, sB.name])

                def ex(kt, i):
                    c0 = cols_for(kt)
                    sA, sB = SP[i % 2]
                    pA, pB = pbufs[i % 3]
                    for hh, (s_, p_) in enumerate(((sA, pA), (sB, pB))):
                        pk = f"{tag}p{i % 3}{hh}"
                        for sb in sbs:
                            o0, o1 = SB_OF[sb]
                            lo = max(o0 - b0, c0)
                            hi = o1 - b0
                            if lo >= hi:
                                continue
                            P.add("act", bind(ex_f, s_, p_, lo, hi, bias_fn(hh, kt, sb)), reads=[s_.name, "btab", "zb"], writes=[pk])
                        if causal and kt >= qt0:
                            P.add("pool", bind(mk_f, p_, (kt - qt0) * 128, min(128, n)), reads=[pk, "cmaskb"], writes=[pk])

                def pv(kt, i):
                    pA, pB = pbufs[i % 3]
                    P.add("pe", bind(pv_f, kt, cols_for(kt), n, pA, pB, kt == 0, kt == last_kt, OA, OBk),
                          reads=[f"{tag}p{i % 3}0", f"{tag}p{i % 3}1", vkey], writes=[OA.name, OBk.name])

                qk(kts[0], 0)
                if len(kts) > 1:
                    qk(kts[1], 1)
                for i, kt in enumerate(kts):
                    ex(kt, i)
                    if i + 2 < len(kts):
                        qk(kts[i + 2], i + 2)
                    pv(kt, i)
                    step[0] += 1
                    want = (step[0] * nfill) // tot_steps
                    while fillers and popped[0] < want:
                        f_ = fillers.pop(0)
                        if f_ is not None:
                            f_()
                        popped[0] += 1
                rall = rcp[:, 0:n]
                rk = tag + "rcp"
                P.add("dve", bind(lambda e, o, i_: e.tensor_copy(out=o, in_=i_), rcp[0:64, 0:n], OBk[0:64, 0:n]), reads=[OBk.name], writes=[rk])
                P.add("dve", bind(lambda e, o, i_: e.tensor_copy(out=o, in_=i_), rcp[64:128, 0:n], OA[64:128, 0:n]), reads=[OA.name], writes=[rk])
                if bi == 0:
                    P.add("dve", bind(lambda e, o: e.tensor_scalar(out=o, in0=o, scalar1=1e-30, scalar2=0.0, op0=ALU.max, op1=ALU.add), rall),
                          reads=[rk], writes=[rk])
                    P.add("dve", bind(lambda e, o: e.reciprocal(out=o, in_=o), rall), reads=[rk], writes=[rk])
                else:
                    P.add("dve", bind(lambda e, o: e.reciprocal(out=o, in_=o), rall), reads=[rk], writes=[rk])
                P.add("dve", bind(lambda e, o, a_, r_: e.tensor_tensor(out=o, in0=a_, in1=r_, op=ALU.mult), dst_fn(slice(0, 64), b0, b1), OA[0:64, 0:n], rcp[64:128, 0:n]),
                      reads=[OA.name, rk], writes=[dkey])
                P.add("dve", bind(lambda e, o, a_, r_: e.tensor_tensor(out=o, in0=a_, in1=r_, op=ALU.mult), dst_fn(slice(64, 128), b0, b1), OBk[64:128, 0:n], rcp[0:64, 0:n]),
                      reads=[OBk.name, rk], writes=[dkey])

            for f_ in fillers:
                if f_ is not None:
                    f_()

        rcp0 = A.alloc(F32, 512)
        pbufs0 = [[A.alloc(BF16, 512), A.alloc(BF16, 512)] for _ in range(3)]
        sqb0 = A.alloc(BF16, 512)
        rsb0 = A.alloc(F32, 512)
        mark2 = A.top
        kTp = [A.alloc(BF16, NCTX) for _ in range(2)]
        qTp0 = [A.alloc(BF16, NOWN) for _ in range(2)]
        Vp = [A.alloc(BF16, 32, 192) for _ in range(2)]
        wp0 = [A.alloc(BF16, 8, 384) for _ in range(2)]
        for i in range(2):
            P.add("pool", lambda e, i=i: e.memset(Vp[i][:, :, 64:128], 1.0), writes=[f"Vp{i}"])
        hT_keys = [f"hT{t}" for t in range(32)]

        def fox_pair_work(p, banks=None):
            i = p % 2
            banks = banks or PSG

            def loads():
                wload(wp0[i][:, :, 0:128], wsrc(w_fox, p * 128, (p + 1) * 128), f"wp{i}q")
                wload(wp0[i][:, :, 128:256], wsrc(w_fox, 768 + p * 128, 768 + (p + 1) * 128), f"wp{i}k")
                wload(wp0[i][:, :, 256:384], wsrc(w_fox, 1536 + p * 128, 1536 + (p + 1) * 128), f"wp{i}v")
            chunks = []

            def kc(cb):
                pt = nextps(banks)
                st_ = qknorm_stages(pt, 512, vecs[:, V_FKG:V_FKG + 1], kTp[i][:, cb * 512:(cb + 1) * 512], f"kTp{i}", sqb0, rsb0, "n0", banks=banks)

                def s1a():
                    proj_half(pt, 512, lambda k: wp0[i][:, k, 128:256], lambda k: hT[:, k, cb * 512:(cb + 1) * 512],
                              hT_keys[4 * cb:4 * cb + 4] + [f"wp{i}k"], 0)

                def s1b():
                    proj_half(pt, 512, lambda k: wp0[i][:, k, 128:256], lambda k: hT[:, k, cb * 512:(cb + 1) * 512],
                              hT_keys[4 * cb:4 * cb + 4] + [f"wp{i}k"], 1)
                return [s1a, s1b, None, st_[0], st_[1], None, st_[2], None, st_[3]]

            def qc(b0, b1):
                pt = nextps(banks)
                n = b1 - b0
                st_ = qknorm_stages(pt, n, vecs[:, V_FQG:V_FQG + 1], qTp0[i][:, b0:b1], f"qTp{i}", sqb0, rsb0, "n0", banks=banks)

                def s1a():
                    proj_half(pt, n, lambda k: wp0[i][:, k, 0:128], lambda k: hT[:, k, OWN0 + b0:OWN0 + b1],
                              hT_keys[(OWN0 + b0) // 128:(OWN0 + b1) // 128] + [f"wp{i}q"], 0)

                def s1b():
                    proj_half(pt, n, lambda k: wp0[i][:, k, 0:128], lambda k: hT[:, k, OWN0 + b0:OWN0 + b1],
                              hT_keys[(OWN0 + b0) // 128:(OWN0 + b1) // 128] + [f"wp{i}q"], 1)
                return [s1a, s1b, None, st_[0], st_[1], None, st_[2], None, st_[3]]

            def vc(t2):
                pt = nextps(banks)

                def vmm(e, j):
                    t = 2 * t2 + j
                    for k in range(8):
                        r = e.matmul(pt[:, j * 128:(j + 1) * 128], lhsT=hT[:, k, t * 128:(t + 1) * 128], rhs=wp0[i][:, k, 256:384],
                                     start=(k == 0), stop=(k == 7))
                    return r

                def s1():
                    P.add("pe", lambda e: vmm(e, 0), reads=hT_keys[2 * t2:2 * t2 + 2] + [f"wp{i}v"], writes=[pt.name])

                def s1b():
                    P.add("pe", lambda e: vmm(e, 1), reads=hT_keys[2 * t2:2 * t2 + 2] + [f"wp{i}v"], writes=[pt.name])
                ptv = pt[:, 0:256].rearrange("p (a b) -> p a b", a=2, b=128)

                def s2():
                    P.add("dve", lambda e: e.tensor_copy(out=Vp[i][:, 2 * t2:2 * t2 + 2, 0:64], in_=ptv[:, :, 0:64]),
                          reads=[pt.name], writes=[f"Vp{i}"])
                    P.add("dve", lambda e: e.tensor_copy(out=Vp[i][:, 2 * t2:2 * t2 + 2, 128:192], in_=ptv[:, :, 64:128]),
                          reads=[pt.name], writes=[f"Vp{i}"])
                return [s1, s1b, None, s2]
            for cb in range(8):
                chunks.extend(kc(cb))
            for (b0, b1) in OB:
                chunks.extend(qc(b0, b1))
            for t2 in range(16):
                chunks.extend(vc(t2))
            return loads, chunks

        zb = A.alloc(F32, 1)
        P.add("dve", lambda e: e.memset(zb, 0.0), writes=["zb"])

        def cross_work(layer, c, w_src, col0, hsrc_fn, hkeys_fn, wpX, qTpX, sqbX, rsbX, banks=None):
            i = c % 2
            banks = banks or PSG

            def loads():
                wload(wpX[i][:, :, 0:128], wsrc(w_src, col0 + c * 128, col0 + (c + 1) * 128), f"wp{i}q")
            chunks = []

            def qc(b0, b1):
                pt = nextps(banks)
                n = b1 - b0
                st_ = qknorm_stages(pt, n, vecs[:, V_CQG + layer:V_CQG + layer + 1], qTpX[i][:, b0:b1], f"qTp{i}", sqbX, rsbX, "n0" if sqbX is sqb0 else "n1", banks=banks)

                def s1():
                    proj(pt, n, lambda k: wpX[i][:, k, 0:128], lambda k: hsrc_fn(k, b0, b1), hkeys_fn(b0, b1) + [f"wp{i}q"])
                return [s1] + st_
            for (b0, b1) in OB:
                chunks.extend(qc(b0, b1))
            return loads, chunks

        def cross_att(layer, c, qTpX, dst, pbX, rcX, zbX, tag, fillers=(), every=2, obanks=None):
            i = c % 2
            attention(qTpX[i], f"qTp{i}", 2, lambda kt: ckT[:, layer, c, kt * 128:(kt + 1) * 128], f"ckT{layer}{c}",
                      lambda kt: (vmem[:, kt, 2 * c, :], vmem[:, kt, 2 * c + 1, :]), "vmem",
                      lambda hh, kt, sb: zbX, False,
                      lambda rows, b0, b1: dst[rows, c, b0:b1], f"attnT{6 + c}", pbX, rcX, tag, fillers=fillers, every=every, obanks=obanks)

        h0src = lambda k, b0, b1: hT[:, k, OWN0 + b0:OWN0 + b1]
        h0keys = lambda b0, b1: hT_keys[(OWN0 + b0) // 128:(OWN0 + b1) // 128]
        attn67 = attnT0[:, 6:8, :]
        l0, c0_ = fox_pair_work(0, banks=list(ps))
        l0()
        for f_ in c0_:
            if f_ is not None:
                f_()
        for p in range(6):
            i = p % 2
            if p < 5:
                nl, nch = fox_pair_work(p + 1)
            else:
                nl, nch = cross_work(0, 0, w_fox, 2316, h0src, h0keys, wp0, qTp0, sqb0, rsb0)
            nl()
            attention(qTp0[i], f"qTp{i}", 32, lambda kt, i=i: kTp[i][:, kt * 128:(kt + 1) * 128], f"kTp{i}",
                      lambda kt, i=i: (Vp[i][:, kt, 0:128], Vp[i][:, kt, 64:192]), f"Vp{i}",
                      lambda hh, kt, sb, p=p: btab[:, 2 * p + hh, kt, sb:sb + 1], True,
                      lambda rows, b0, b1, p=p: attnT0[rows, p, b0:b1], f"attnT{p}", pbufs0, rcp0, "b0", fillers=nch, every=2)
        nl, nch = cross_work(0, 1, w_fox, 2316, h0src, h0keys, wp0, qTp0, sqb0, rsb0)
        nl()
        cross_att(0, 0, qTp0, attn67, pbufs0, rcp0, zb, "b0", fillers=nch, every=1)
        cross_att(0, 1, qTp0, attn67, pbufs0, rcp0, zb, "b0", obanks=[(ps[4], ps[5]), (ps[6], ps[7])])
        dump("attnT0", attnT0[:, 0, :], ["attnT0"])
        dump("attnT6", attnT0[:, 6, :], ["attnT6"])

        attn_keys = [f"attnT{c}" for c in range(8)]

        def outproj_block(wo, src_fn, src_keys, bi):
            b0, b1 = OB[bi]
            n = b1 - b0
            for dc in range(8):
                pt = nextps(PSALL)
                proj(pt, n, lambda k, dc=dc: wo[:, k, dc * 128:(dc + 1) * 128], lambda k: src_fn(k, b0, b1), src_keys + [f"wo{dc // 4}"])
                P.add("dve", lambda e, pt=pt, dc=dc: e.tensor_tensor(out=xT[:, dc, b0:b1], in0=xT[:, dc, b0:b1], in1=pt[:, 0:n], op=ALU.add),
                      reads=[pt.name, f"xT{dc}"], writes=[f"xT{dc}"])

        def outproj(layer, wo, src_fn, src_keys):
            for half in range(2):
                wload(wo[:, :, half * 512:(half + 1) * 512], wsrc(w_out[layer], half * 512, (half + 1) * 512), f"wo{half}")
            for dc in range(8):
                for (b0, b1) in OB:
                    pt = nextps(PSALL)
                    n = b1 - b0
                    proj(pt, n, lambda k, dc=dc: wo[:, k, dc * 128:(dc + 1) * 128], lambda k, b0=b0, b1=b1: src_fn(k, b0, b1),
                         src_keys + [f"wo{dc // 4}"])
                    P.add("dve", lambda e, pt=pt, dc=dc, b0=b0, b1=b1, n=n: e.tensor_tensor(out=xT[:, dc, b0:b1], in0=xT[:, dc, b0:b1], in1=pt[:, 0:n], op=ALU.add),
                          reads=[pt.name, f"xT{dc}"], writes=[f"xT{dc}"])

        P.barrier(lambda e: e.memset(onesf[:, 0:8], 1.0))
        A.reset()
        A.limit = XT_OFF
        assert A.top + 8000 < mark1
        PSALL = ps
        wo = A.alloc(BF16, 8, D)
        xs2 = [A.alloc(F32, D) for _ in range(3)]
        for t in range(17):
            xb_ = xs2[t % 3]
            nr = min(128, NOWN - t * 128)
            P.add("sp", lambda e, t=t, xb_=xb_, nr=nr: e.dma_start(out=xb_[0:nr, :], in_=xctx[OWN0 + t * 128:OWN0 + t * 128 + nr, :]), writes=[f"xs2{t % 3}"], dma=True)
            for k2 in range(2):
                pt = nextps(PSALL)

                def tr2(e, xb_=xb_, pt=pt, k2=k2, nr=nr):
                    for j in range(4):
                        k = 4 * k2 + j
                        r = e.transpose(pt[:, j * 128:j * 128 + nr], xb_[0:nr, k * 128:(k + 1) * 128], identf[0:nr, 0:nr])
                    return r
                P.add("pe", tr2, reads=[f"xs2{t % 3}", "identf"], writes=[pt.name])
                src = pt[:, :].rearrange("p (a b) -> p a b", a=4, b=128)[:, :, 0:nr]
                dst = xT[:, 4 * k2:4 * k2 + 4, t * 128:t * 128 + nr]
                if k2 == 0:
                    P.add("act", lambda e, src=src, dst=dst: e.copy(out=dst, in_=src),
                          reads=[pt.name], writes=[f"xT{4 * k2 + j}" for j in range(4)])
                else:
                    P.add("dve", lambda e, src=src, dst=dst: e.tensor_copy(out=dst, in_=src),
                          reads=[pt.name], writes=[f"xT{4 * k2 + j}" for j in range(4)])
        outproj(0, wo, lambda k, b0, b1: attnT0[:, k, b0:b1], attn_keys)
        dump("x0p", xT[:, 0, :], ["xT0"])

        def rms_feat(gcol0, hdst, hkey, tmp_sq, tmp_rs):
            pts = {}

            def stats(bi):
                b0, b1 = OB[bi]
                n = b1 - b0
                pt = nextps(PSALL)
                pts[bi] = pt
                for k in range(8):
                    P.add("act", lambda e, k=k: e.activation(out=tmp_sq[:, k, 0:n], in_=xT[:, k, b0:b1], func=AF.Square),
                          reads=[f"xT{k}"], writes=[f"tsq{k}"])
                    P.add("pe", lambda e, k=k: e.matmul(pt[:, 0:n], lhsT=onesb, rhs=tmp_sq[:, k, 0:n], start=(k == 0), stop=(k == 7)),
                          reads=[f"tsq{k}", "onesb"], writes=[pt.name])

            def norm(bi):
                b0, b1 = OB[bi]
                n = b1 - b0
                pt = pts[bi]
                rs = tmp_rs[bi % 2]
                rk = f"trs{bi % 2}"
                rstd_ps(pt[:, 0:n], 1.0 / D, rs[:, 0:n], [pt.name], rk, "act")
                for k in range(8):
                    P.add("dve", lambda e, k=k: e.scalar_tensor_tensor(out=hdst(k, b0, b1), in0=xT[:, k, b0:b1],
                                                                       scalar=vecs[:, gcol0 + k:gcol0 + k + 1], in1=rs[:, 0:n],
                                                                       op0=ALU.mult, op1=ALU.mult),
                          reads=[f"xT{k}", rk, "vecs"], writes=[f"{hkey}{bi}"])
            stats(0)
            for bi in range(5):
                if bi + 1 < 5:
                    stats(bi + 1)
                norm(bi)

        FB = [(0, 510), (510, 1020), (1020, 1530), (1530, 2040), (2040, NOWN)]
        GSZ = 4

        def ffn(layer):
            P.barrier(lambda e: e.memset(onesf[:, 0:8], 1.0))
            A.reset()
            h2 = A.alloc(BF16, 8, NOWN + 2)
            tsq = A.alloc(BF16, 8, 512)
            trs = [A.alloc(F32, 512) for _ in range(2)]
            actT = A.alloc(BF16, GSZ, NOWN)
            wu = [A.alloc(BF16, 8, 2 * GSZ * 128) for _ in range(2)]
            wd = [A.alloc(BF16, GSZ, D) for _ in range(2)]
            ta = [A.alloc(F32, 512) for _ in range(2)]
            tg = [A.alloc(F32, 512) for _ in range(2)]
            tsl = [A.alloc(F32, 512) for _ in range(2)]
            P.add("pool", lambda e: e.memset(h2[:, :, 0:2], 0.0), writes=["h2pad"])
            rms_feat(V_FFNG + 8 * layer, lambda k, b0, b1: h2[:, k, 2 + b0:2 + b1], "h2b", tsq, trs)
            P.add("pool", lambda e: e.tensor_tensor(out=h2[:, :, 2:2 + HALO], in0=h2[:, :, 2:2 + HALO],
                                                    in1=vmaskb[:, 0:HALO].unsqueeze(1).to_broadcast([128, 8, HALO]), op=ALU.mult),
                  reads=["h2b0", "vmaskb"], writes=["h2b0"])
            def h2k(f0, f1):
                ks = ["h2pad"]
                for bi, (b0, b1) in enumerate(OB):
                    if b0 < f1 and b1 > f0 - 2:
                        ks.append(f"h2b{bi}")
                return ks
            ngrp = (22 + GSZ - 1) // GSZ
            cstate = {"cnt": 0}

            def gload(g):
                i = g % 2
                j0 = g * GSZ
                gs = min(GSZ, 22 - j0)
                for jj in range(gs):
                    j = j0 + jj
                    wload(wu[i][:, :, jj * 128:(jj + 1) * 128], wsrc(w_up[layer], j * 128, (j + 1) * 128), f"wu{i}a{jj}")
                    wload(wu[i][:, :, (GSZ + jj) * 128:(GSZ + jj + 1) * 128], wsrc(w_up[layer], FFN + j * 128, FFN + (j + 1) * 128), f"wu{i}g{jj}")
                wload(wd[i][:, 0:gs, :], w_dn[layer][j0 * 128:(j0 + gs) * 128, :].rearrange("(j p) d -> p j d", p=128), f"wd{i}")

            def make_unit(g, jj, f0, f1):
                i = g % 2
                j = g * GSZ + jj
                n = f1 - f0
                st_ = {}

                def front():
                    pa = nextps(PSALL)
                    pg = nextps(PSALL)
                    proj(pa, n + 2, lambda k: wu[i][:, k, jj * 128:(jj + 1) * 128],
                         lambda k: h2[:, k, f0:f1 + 2], h2k(f0, f1) + [f"wu{i}a{jj}"])
                    proj(pg, n + 2, lambda k: wu[i][:, k, (GSZ + jj) * 128:(GSZ + jj + 1) * 128],
                         lambda k: h2[:, k, f0:f1 + 2], h2k(f0, f1) + [f"wu{i}g{jj}"])
                    c = cstate["cnt"] % 2
                    cstate["cnt"] += 1
                    st_["c"] = c
                    for (pp, tt, col, nm) in ((pa, ta[c], j, "ta"), (pg, tg[c], 22 + j, "tg")):
                        fc = fcv[:, layer * 44 + col, :]
                        P.add("act", lambda e, pp=pp, tt=tt, fc=fc: e.activation(out=tt[:, 0:n], in_=pp[:, 2:n + 2], func=AF.Copy, scale=fc[:, 2:3]),
                              reads=[pp.name, "fcv"], writes=[f"{nm}{c}"])
                        P.add("dve", lambda e, pp=pp, tt=tt, fc=fc: e.scalar_tensor_tensor(out=tt[:, 0:n], in0=pp[:, 1:n + 1], scalar=fc[:, 1:2], in1=tt[:, 0:n],
                                                                                         op0=ALU.mult, op1=ALU.add),
                              reads=[pp.name, f"{nm}{c}", "fcv"], writes=[f"{nm}{c}"])
                        P.add("dve", lambda e, pp=pp, tt=tt, fc=fc: e.scalar_tensor_tensor(out=tt[:, 0:n], in0=pp[:, 0:n], scalar=fc[:, 0:1], in1=tt[:, 0:n],
                                                                                         op0=ALU.mult, op1=ALU.add),
                              reads=[pp.name, f"{nm}{c}", "fcv"], writes=[f"{nm}{c}"])

                def tail():
                    c = st_["c"]
                    P.add("act", lambda e: e.activation(out=tsl[c][:, 0:n], in_=tg[c][:, 0:n], func=AF.Silu),
                          reads=[f"tg{c}"], writes=[f"tsl{c}"])
                    P.add("pool", lambda e: e.tensor_tensor(out=actT[:, jj, f0:f1], in0=ta[c][:, 0:n], in1=tsl[c][:, 0:n], op=ALU.mult),
                          reads=[f"ta{c}", f"tsl{c}"], writes=[f"actT{jj}"])
                return front, tail

            def units_of(g):
                gs = min(GSZ, 22 - g * GSZ)
                return [make_unit(g, jj, f0, f1) for jj in range(gs) for (f0, f1) in FB]

            def down(g):
                i = g % 2
                gs = min(GSZ, 22 - g * GSZ)
                for dc in range(8):
                    for (b0, b1) in OB:
                        n = b1 - b0
                        pt = nextps(PSALL)

                        def dmm(e, pt=pt, n=n, dc=dc, b0=b0, b1=b1):
                            for jj in range(gs):
                                r = e.matmul(pt[:, 0:n], lhsT=wd[i][:, jj, dc * 128:(dc + 1) * 128], rhs=actT[:, jj, b0:b1], start=(jj == 0), stop=(jj == gs - 1))
                            return r
                        P.add("pe", dmm, reads=[f"actT{jj}" for jj in range(gs)] + [f"wd{i}"], writes=[pt.name])
                        P.add("dve", lambda e, pt=pt, dc=dc, b0=b0, b1=b1, n=n: e.tensor_tensor(out=xT[:, dc, b0:b1], in0=xT[:, dc, b0:b1], in1=pt[:, 0:n], op=ALU.add),
                              reads=[pt.name, f"xT{dc}"], writes=[f"xT{dc}"])

            LA = 2
            gload(0)
            cur = units_of(0)
            skip = 0
            pend = []
            for g in range(ngrp):
                if g + 1 < ngrp:
                    gload(g + 1)
                for (front, tail) in cur[skip:]:
                    front()
                    if pend:
                        pend.pop()()
                    pend.append(tail)
                nxt = units_of(g + 1) if g + 1 < ngrp else None
                held = []
                if nxt:
                    f1_, t1_ = nxt[0]
                    f1_()
                    pend.pop()()
                    f2_, t2_ = nxt[1]
                    f2_()
                    held = [t1_, t2_]
                else:
                    pend.pop()()
                down(g)
                if nxt:
                    held[0]()
                    pend.append(held[1])
                    cur = nxt
                    skip = LA

        ffn(0)
        dump("x1", xT[:, 0, :], ["xT0"])

        P.barrier(lambda e: e.memset(onesf[:, 0:8], 1.0))
        A.reset()
        h3 = A.alloc(BF16, 8, NOWN)
        at67 = A.alloc(BF16, 2, NOWN)
        vT = A.alloc(BF16, 6, NOWN)
        markL1 = A.top
        tsq1 = A.alloc(BF16, 8, 512)
        trs1 = [A.alloc(F32, 512) for _ in range(2)]
        rcp1 = A.alloc(F32, 512)
        pbufs1 = [[A.alloc(BF16, 512), A.alloc(BF16, 512)] for _ in range(3)]
        sqb1 = A.alloc(BF16, 512)
        rsb1 = A.alloc(F32, 512)
        qTp1 = [A.alloc(BF16, NOWN) for _ in range(2)]
        wp1 = [A.alloc(BF16, 8, 128) for _ in range(2)]
        rms_feat(V_MIXG + 8, lambda k, b0, b1: h3[:, k, b0:b1], "h3b", tsq1, trs1)
        h3keys = [f"h3b{bi}" for bi in range(5)]
        zb1 = A.alloc(F32, 1)
        P.add("dve", lambda e: e.memset(zb1, 0.0), writes=["zb"])

        def attnT1(k, b0, b1):
            return h3[:, k, b0:b1] if k < 6 else at67[:, k - 6, b0:b1]

        h3src = lambda k, b0, b1: h3[:, k, b0:b1]
        h3kf = lambda b0, b1: h3keys
        for c in range(2):
            nl, nch = cross_work(1, c, w_cv, 1536, h3src, h3kf, wp1, qTp1, sqb1, rsb1, banks=list(ps))
            nl()
            for f_ in nch:
                f_()
            cross_att(1, c, qTp1, at67, pbufs1, rcp1, zb1, "b1", obanks=[(ps[4], ps[5]), (ps[6], ps[7])])
        P.barrier(lambda e: e.memset(onesf[:, 0:8], 1.0))
        A.top = markL1
        up_ = [A.alloc(BF16, NOWN + 30) for _ in range(2)]
        dg = [A.alloc(BF16, 31, 128) for _ in range(2)]
        wcv = [A.alloc(BF16, 8, 256) for _ in range(2)]
        sg = [A.alloc(F32, 512) for _ in range(2)]
        cacc = [A.alloc(F32, 512) for _ in range(2)]
        ccnt = [0]
        NPE = 27
        for i in range(2):
            P.add("pool", lambda e, i=i: e.memset(up_[i][:, 0:30], 0.0), writes=[f"up{i}pad"])
        for c in range(6):
            i = c % 2
            wload(wcv[i][:, :, 0:128], wsrc(w_cv, c * 128, (c + 1) * 128), f"wcv{i}a")
            wload(wcv[i][:, :, 128:256], wsrc(w_cv, 768 + c * 128, 768 + (c + 1) * 128), f"wcv{i}g")
            for k in range(NPE):
                P.add("pool", lambda e, i=i, c=c, k=k: e.tensor_scalar(out=dg[i][:, k, :], in0=identb, scalar1=cw[:, c, k:k + 1], scalar2=0.0, op0=ALU.mult, op1=ALU.add),
                      reads=["identb", "cw"], writes=[f"dg{i}"])
            for bi, (b0, b1) in enumerate(OB):
                n = b1 - b0
                pa = nextps(PSALL)
                pg = nextps(PSALL)
                proj(pa, n, lambda k, i=i: wcv[i][:, k, 0:128], lambda k, b0=b0, b1=b1: h3[:, k, b0:b1], [h3keys[bi], f"wcv{i}a"])
                proj(pg, n, lambda k, i=i: wcv[i][:, k, 128:256], lambda k, b0=b0, b1=b1: h3[:, k, b0:b1], [h3keys[bi], f"wcv{i}g"])
                s_ = sg[bi % 2]
                P.add("act", lambda e, pg=pg, s_=s_, n=n: e.activation(out=s_[:, 0:n], in_=pg[:, 0:n], func=AF.Sigmoid),
                      reads=[pg.name], writes=[f"sg{bi % 2}"])
                P.add("dve", lambda e, pa=pa, s_=s_, n=n, i=i, b0=b0, b1=b1: e.tensor_tensor(out=up_[i][:, 30 + b0:30 + b1], in0=pa[:, 0:n], in1=s_[:, 0:n], op=ALU.mult),
                      reads=[pa.name, f"sg{bi % 2}"], writes=[f"up{i}b{bi}"])
                if bi == 0:
                    P.add("pool", lambda e, i=i: e.tensor_tensor(out=up_[i][:, 30:30 + HALO], in0=up_[i][:, 30:30 + HALO], in1=vmaskb[:, 0:HALO], op=ALU.mult),
                          reads=[f"up{i}b0", "vmaskb"], writes=[f"up{i}b0"])
            for bi, (b0, b1) in enumerate(OB):
                n = b1 - b0
                pt = nextps(PSALL)

                def cmm(e, pt=pt, n=n, i=i, b0=b0, b1=b1):
                    for k in range(NPE):
                        r = e.matmul(pt[:, 0:n], lhsT=dg[i][:, k, :], rhs=up_[i][:, b0 + k:b1 + k], start=(k == 0), stop=(k == NPE - 1))
                    return r
                ukeys = [f"up{i}b{x}" for x in range(max(0, bi - 1), bi + 1)] + [f"up{i}pad"]
                P.add("pe", cmm, reads=ukeys + [f"dg{i}"], writes=[pt.name])
                ac = cacc[ccnt[0] % 2]
                ak = f"cacc{ccnt[0] % 2}"
                ccnt[0] += 1
                for k in range(NPE, 31):
                    if k == NPE:
                        P.add("dve", lambda e, ac=ac, n=n, i=i, b0=b0, b1=b1, c=c, k=k: e.tensor_scalar(
                            out=ac[:, 0:n], in0=up_[i][:, b0 + k:b1 + k], scalar1=cw[:, c, k:k + 1], scalar2=vecs[:, V_CVB + c:V_CVB + c + 1],
                            op0=ALU.mult, op1=ALU.add), reads=ukeys + ["cw", "vecs"], writes=[ak])
                    else:
                        P.add("dve", lambda e, ac=ac, n=n, i=i, b0=b0, b1=b1, c=c, k=k: e.scalar_tensor_tensor(
                            out=ac[:, 0:n], in0=up_[i][:, b0 + k:b1 + k], scalar=cw[:, c, k:k + 1], in1=ac[:, 0:n],
                            op0=ALU.mult, op1=ALU.add), reads=ukeys + ["cw", ak], writes=[ak])
                P.add("dve", lambda e, pt=pt, ac=ac, n=n, c=c, b0=b0, b1=b1: e.tensor_tensor(out=vT[:, c, b0:b1], in0=pt[:, 0:n], in1=ac[:, 0:n], op=ALU.add),
                      reads=[pt.name, ak], writes=[f"vT{c}b{bi}"])
        P.barrier(lambda e: e.memset(onesf[:, 0:8], 1.0))
        A.top = markL1
        vsq = A.alloc(BF16, 6, 512)
        lmean = A.alloc(F32, 512)
        lm2 = A.alloc(F32, 512)
        lrs = A.alloc(F32, 512)
        ltmp = [A.alloc(F32, 512) for _ in range(2)]
        wo1 = A.alloc(BF16, 8, D)
        for bi, (b0, b1) in enumerate(OB):
            n = b1 - b0
            p1 = nextps(PSALL)
            p2 = nextps(PSALL)
            for c in range(6):
                P.add("act", lambda e, c=c, b0=b0, b1=b1, n=n: e.activation(out=vsq[:, c, 0:n], in_=vT[:, c, b0:b1], func=AF.Square),
                      reads=[f"vT{c}b{bi}"], writes=[f"vsq{c}"])
                P.add("pe", lambda e, c=c, b0=b0, b1=b1, n=n, p1=p1: e.matmul(p1[:, 0:n], lhsT=onesb, rhs=vT[:, c, b0:b1], start=(c == 0), stop=(c == 5)),
                      reads=[f"vT{c}b{bi}", "onesb"], writes=[p1.name])
                P.add("pe", lambda e, c=c, n=n, p2=p2: e.matmul(p2[:, 0:n], lhsT=onesb, rhs=vsq[:, c, 0:n], start=(c == 0), stop=(c == 5)),
                      reads=[f"vsq{c}", "onesb"], writes=[p2.name])
            P.add("act", lambda e, p1=p1, n=n: e.activation(out=lmean[:, 0:n], in_=p1[:, 0:n], func=AF.Copy, scale=1.0 / 768), reads=[p1.name], writes=["lmean"])
            P.add("dve", lambda e, n=n: e.tensor_tensor(out=lm2[:, 0:n], in0=lmean[:, 0:n], in1=lmean[:, 0:n], op=ALU.mult), reads=["lmean"], writes=["lm2"])
            P.add("dve", lambda e, p2=p2, n=n: e.scalar_tensor_tensor(out=lm2[:, 0:n], in0=p2[:, 0:n], scalar=1.0 / 768, in1=lm2[:, 0:n], op0=ALU.mult, op1=ALU.subtract),
                  reads=[p2.name, "lm2"], writes=["lm2"])
            P.add("act", lambda e, n=n: e.activation(out=lrs[:, 0:n], in_=lm2[:, 0:n], func=AF.Sqrt, bias=epsb), reads=["lm2", "epsb"], writes=["lrs"])
            P.add("dve", lambda e, n=n: e.reciprocal(out=lrs[:, 0:n], in_=lrs[:, 0:n]), reads=["lrs"], writes=["lrs"])
            for c in range(6):
                lt = ltmp[c % 2]
                P.add("dve", lambda e, c=c, lt=lt, b0=b0, b1=b1, n=n: e.tensor_tensor(out=lt[:, 0:n], in0=vT[:, c, b0:b1], in1=lmean[:, 0:n], op=ALU.subtract),
                      reads=[f"vT{c}b{bi}", "lmean"], writes=[f"ltmp{c % 2}"])
                P.add("pool", lambda e, lt=lt, n=n: e.tensor_tensor(out=lt[:, 0:n], in0=lt[:, 0:n], in1=lrs[:, 0:n], op=ALU.mult),
                      reads=[f"ltmp{c % 2}", "lrs"], writes=[f"ltmp{c % 2}"])
                P.add("act", lambda e, c=c, lt=lt, b0=b0, b1=b1, n=n: e.activation(out=h3[:, c, b0:b1], in_=lt[:, 0:n], func=AF.Silu,
                                                                                  scale=vecs[:, V_LNG + c:V_LNG + c + 1], bias=vecs[:, V_LNB + c:V_LNB + c + 1]),
                      reads=[f"ltmp{c % 2}", "vecs"] + h3keys, writes=[f"tok{c}b{bi}"])
            if bi == 0:
                for half in range(2):
                    wload(wo1[:, :, half * 512:(half + 1) * 512], wsrc(w_out[1], half * 512, (half + 1) * 512), f"wo{half}")
            if bi >= 1:
                outproj_block(wo1, attnT1, [f"tok{c}b{bi - 1}" for c in range(6)] + ["attnT6", "attnT7"], bi - 1)
        outproj_block(wo1, attnT1, [f"tok{c}b4" for c in range(6)] + ["attnT6", "attnT7"], 4)

        dump("x1p", xT[:, 0, :], ["xT0"])

        ffn(1)

        P.barrier(lambda e: e.memset(onesf[:, 0:8], 1.0))
        A.reset()
        ob = [A.alloc(F32, D) for _ in range(3)]
        for t in range(16):
            o_ = ob[t % 3]
            for k2 in range(2):
                pt = nextps(PSALL)

                def tr3(e, pt=pt, t=t, k2=k2):
                    for j in range(4):
                        k = 4 * k2 + j
                        r = e.transpose(pt[:, j * 128:(j + 1) * 128], xT[:, k, HALO + t * 128:HALO + (t + 1) * 128], identf)
                    return r
                P.add("pe", tr3, reads=[f"xT{4 * k2 + j}" for j in range(4)] + ["identf"], writes=[pt.name])
                if k2 == 0:
                    P.add("act", lambda e, pt=pt, o_=o_: e.copy(out=o_[:, 0:512], in_=pt[:, :]), reads=[pt.name], writes=[f"ob{t % 3}a"])
                else:
                    P.add("dve", lambda e, pt=pt, o_=o_: e.tensor_copy(out=o_[:, 512:1024], in_=pt[:, :]), reads=[pt.name], writes=[f"ob{t % 3}b"])
            P.add("sp", lambda e, t=t, o_=o_: e.dma_start(out=out_d[t * 128:(t + 1) * 128, :], in_=o_), reads=[f"ob{t % 3}a", f"ob{t % 3}b"], dma=True, out=True)

        P.emit(nc, st)
    return nc


def _cols(v):
    v = np.asarray(v, np.float32)
    return np.ascontiguousarray(v.reshape(-1, 128).T)


def _rep64(v):
    v = np.asarray(v, np.float32)
    return np.concatenate([v, v])[:, None]


def make_in_maps(inp):
    x = np.asarray(inp["x"], np.float32)
    mem = np.asarray(inp["mem"], np.float32)
    vecs = np.zeros((128, 160), np.float32)
    vecs[:, 0:8] = _cols(inp["mix_norm_g"][0])
    vecs[:, 8:16] = _cols(inp["mix_norm_g"][1])
    vecs[:, 16:24] = _cols(inp["ffn_norm_g"][0])
    vecs[:, 24:32] = _cols(inp["ffn_norm_g"][1])
    vecs[:, 32:40] = _cols(inp["mem_norm_g"])
    vecs[:, 40:41] = _rep64(inp["cross_q_g"][0])
    vecs[:, 41:42] = _rep64(inp["cross_q_g"][1])
    vecs[:, 42:43] = _rep64(inp["cross_k_g"][0])
    vecs[:, 43:44] = _rep64(inp["cross_k_g"][1])
    vecs[:, 44:45] = _rep64(inp["fox_q_g"][0])
    vecs[:, 45:46] = _rep64(inp["fox_k_g"][0])
    vecs[:, 46:52] = _cols(inp["conv_dw_b"][0])
    vecs[:, 52:58] = _cols(inp["conv_ln_g"][0])
    vecs[:, 58:64] = _cols(inp["conv_ln_b"][0])
    vecs[0:12, 64] = np.asarray(inp["fox_b_f"], np.float32)[0]
    cwv = np.asarray(inp["conv_dw"], np.float32)[0]
    cw = np.ascontiguousarray(cwv.reshape(31, 6, 128).transpose(2, 1, 0)).reshape(128, 6 * 31)
    fc = np.asarray(inp["ffn_conv"], np.float32)
    fcv = np.ascontiguousarray(fc.reshape(2, 3, 44, 128).transpose(3, 0, 2, 1)).reshape(128, 2 * 44 * 3)
    identf = np.eye(128, dtype=np.float32)
    cmask = np.triu(np.ones((128, 128), np.float32))
    shared = {
        "vecs": vecs, "cw": cw, "fcv": fcv, "identf": identf, "cmask": cmask,
        "mem_w_kv": np.asarray(inp["mem_w_kv"], np.float32),
        "mix_w_out": np.asarray(inp["mix_w_out"], np.float32),
        "fox_w_in": np.ascontiguousarray(np.asarray(inp["fox_w_in"], np.float32)[0]),
        "conv_w_in": np.ascontiguousarray(np.asarray(inp["conv_w_in"], np.float32)[0]),
        "ffn_w_up": np.asarray(inp["ffn_w_up"], np.float32),
        "ffn_w_down": np.asarray(inp["ffn_w_down"], np.float32),
    }
    maps = []
    for core in range(8):
        b, h = divmod(core, 2)
        if h == 0:
            xc = np.concatenate([np.zeros((2048, D), np.float32), x[b, 0:2048]], axis=0)
            km = np.zeros((128, 32), np.float32)
            km[:, 0:16] = NEG
            vm = np.zeros((128, 128), np.float32)
        else:
            xc = x[b]
            km = np.zeros((128, 32), np.float32)
            vm = np.ones((128, 128), np.float32)
        m = dict(shared)
        m["xctx"] = np.ascontiguousarray(xc)
        m["mem"] = np.ascontiguousarray(mem[b])
        m["kmask"] = km
        m["vmask"] = vm
        maps.append(m)
    return maps


_NC_CACHE = {}


def kernel(**inputs):
    if "nc" not in _NC_CACHE:
        _NC_CACHE["nc"] = build_nc()
    nc = _NC_CACHE["nc"]
    maps = make_in_maps(inputs)
    res = run_bass_kernel_spmd(nc, maps, core_ids=list(range(8)))
    out = np.zeros((4, 4096, D), np.float32)
    for core in range(8):
        b, h = divmod(core, 2)
        out[b, h * 2048:(h + 1) * 2048] = res.results[core]["out"]
    return out
```
